# Optimizing a Trainium2 kernel written in Bass

```python
import math
import jax, jax.numpy as jnp
from jax import lax
import numpy as np

D_MODEL = 4096
BATCH = 8
SEQ = 2048
DEPTH = 4

CHUNK = 64
Q_BLOCK = 128
N_A_LAYERS = DEPTH // 2
N_B_LAYERS = DEPTH - N_A_LAYERS
SB_HEAD_DIM = 128
SB_HEADS = D_MODEL // SB_HEAD_DIM
DIFF_HEAD_DIM = 128
DIFF_HEADS = D_MODEL // (2 * DIFF_HEAD_DIM)
D_FF = 3 * D_MODEL // 2
RMS_EPS = 1e-6
LAMBDA_STD = 0.1
NORM_NOISE = 0.01

kernel_name = "yoco_stickbreak_diffattn_macaron"


def rms_norm(x, g):
    xf = x.astype(jnp.float32)
    y = xf * lax.rsqrt(jnp.mean(xf * xf, axis=-1, keepdims=True) + RMS_EPS)
    return y.astype(x.dtype) * g


def swiglu(x, w_in, w_out):
    gate, up = jnp.split(x @ w_in, 2, axis=-1)
    return (jax.nn.silu(gate) * up) @ w_out


def macaron_half(h, g, w_in, w_out):
    return h + 0.5 * swiglu(rms_norm(h, g), w_in, w_out)


def stick_breaking_attention(q, k, v):
    seq = q.shape[1]
    scale = 1.0 / math.sqrt(SB_HEAD_DIM)
    outs = []
    for q0 in range(0, seq, Q_BLOCK):
        lk = q0 + Q_BLOCK
        z = jnp.einsum('bqhd,bkhd->bhqk', q[:, q0:lk], k[:, :lk]).astype(jnp.float32) * scale
        t = q0 + jnp.arange(Q_BLOCK)[:, None]
        s = jnp.arange(lk)[None, :]
        strict = s < t
        log_beta = jax.nn.log_sigmoid(z)
        log_1m = jnp.where(strict, jax.nn.log_sigmoid(-z), 0.0)
        later = lax.cumsum(log_1m, axis=3, reverse=True) - log_1m
        a = jnp.where(strict, jnp.exp(log_beta + later), 0.0)
        outs.append(jnp.einsum('bhqk,bkhd->bqhd', a.astype(v.dtype), v[:, :lk]))
    return jnp.concatenate(outs, axis=1)


def alibi_slopes(n_heads):
    return 2.0 ** (-8.0 * jnp.arange(1, n_heads + 1, dtype=jnp.float32) / n_heads)


def differential_attention(q, k, v, lam):
    seq = q.shape[1]
    scale = 1.0 / math.sqrt(DIFF_HEAD_DIM)
    slopes = alibi_slopes(DIFF_HEADS)[:, None, None]
    outs = []
    for q0 in range(0, seq, Q_BLOCK):
        lk = q0 + Q_BLOCK
        scores = jnp.einsum('bqhcd,bkhcd->bchqk', q[:, q0:lk], k[:, :lk]).astype(jnp.float32) * scale
        t = q0 + jnp.arange(Q_BLOCK)[:, None]
        s = jnp.arange(lk)[None, :]
        bias = -slopes * jnp.abs(t - s).astype(jnp.float32)
        allowed = (s // CHUNK) <= (t // CHUNK)
        p = jax.nn.softmax(jnp.where(allowed, scores + bias, -jnp.inf), axis=-1)
        attn = p[:, 0] - lam * p[:, 1]
        outs.append(jnp.einsum('bhqk,bkhe->bqhe', attn.astype(v.dtype), v[:, :lk]))
    return jnp.concatenate(outs, axis=1)


def setup_inputs(seed: int = 0) -> dict:
    key = jax.random.key(seed)
    ks = jax.random.split(key, 17)
    f32 = jnp.float32
    D, F = D_MODEL, D_FF

    def w(k, shape, fan_in):
        return jax.random.normal(k, shape, f32) * (fan_in ** -0.5)

    def gain(k, shape):
        return 1.0 + NORM_NOISE * jax.random.normal(k, shape, f32)

    return {
        'x': jax.random.normal(ks[0], (BATCH, SEQ, D), f32),
        'ffn_norm': gain(ks[1], (DEPTH, 2, D)),
        'w_ffn_in': w(ks[2], (DEPTH, 2, D, 2 * F), D),
        'w_ffn_out': w(ks[3], (DEPTH, 2, F, D), F),
        'attn_norm': gain(ks[4], (DEPTH, D)),
        'w_qkv_a': w(ks[5], (N_A_LAYERS, D, 3 * D), D),
        'w_o_a': w(ks[6], (N_A_LAYERS, D, D), D),
        'kv_norm': gain(ks[7], (D,)),
        'w_kv_b': w(ks[8], (D, 2 * D), D),
        'w_q_b': w(ks[9], (N_B_LAYERS, D, D), D),
        'lambda_q1': LAMBDA_STD * jax.random.normal(ks[10], (N_B_LAYERS, DIFF_HEAD_DIM), f32),
        'lambda_k1': LAMBDA_STD * jax.random.normal(ks[11], (N_B_LAYERS, DIFF_HEAD_DIM), f32),
        'lambda_q2': LAMBDA_STD * jax.random.normal(ks[12], (N_B_LAYERS, DIFF_HEAD_DIM), f32),
        'lambda_k2': LAMBDA_STD * jax.random.normal(ks[13], (N_B_LAYERS, DIFF_HEAD_DIM), f32),
        'subln_norm': gain(ks[14], (N_B_LAYERS, 2 * DIFF_HEAD_DIM)),
        'w_o_b': w(ks[15], (N_B_LAYERS, D, D), D),
        'final_norm': gain(ks[16], (D,)),
    }


def reference(x, ffn_norm, w_ffn_in, w_ffn_out, attn_norm, w_qkv_a, w_o_a, kv_norm,
              w_kv_b, w_q_b, lambda_q1, lambda_k1, lambda_q2, lambda_k2, subln_norm,
              w_o_b, final_norm):
    B, S, D = x.shape
    f32 = jnp.float32
    h = x
    kv_shared = None
    for l in range(DEPTH):
        h = macaron_half(h, ffn_norm[l, 0], w_ffn_in[l, 0], w_ffn_out[l, 0])
        xn = rms_norm(h, attn_norm[l])
        if l < N_A_LAYERS:
            q, k, v = jnp.split(xn @ w_qkv_a[l], 3, axis=-1)
            shp = (B, S, SB_HEADS, SB_HEAD_DIM)
            o = stick_breaking_attention(q.reshape(shp), k.reshape(shp), v.reshape(shp))
            h = h + o.reshape(B, S, D) @ w_o_a[l]
        else:
            j = l - N_A_LAYERS
            lam_init = 0.8 - 0.6 * math.exp(-0.3 * l)
            q = (xn @ w_q_b[j]).reshape(B, S, DIFF_HEADS, 2, DIFF_HEAD_DIM)
            lam = (jnp.exp(jnp.sum(lambda_q1[j].astype(f32) * lambda_k1[j].astype(f32)))
                   - jnp.exp(jnp.sum(lambda_q2[j].astype(f32) * lambda_k2[j].astype(f32)))
                   + lam_init)
            k_sh, v_sh = kv_shared
            o = differential_attention(q, k_sh, v_sh, lam)
            o = rms_norm(o, subln_norm[j]) * (1.0 - lam_init)
            h = h + o.reshape(B, S, D) @ w_o_b[j]
        h = macaron_half(h, ffn_norm[l, 1], w_ffn_in[l, 1], w_ffn_out[l, 1])
        if l == N_A_LAYERS - 1:
            k_sh, v_sh = jnp.split(rms_norm(h, kv_norm) @ w_kv_b, 2, axis=-1)
            kv_shared = (k_sh.reshape(B, S, DIFF_HEADS, 2, DIFF_HEAD_DIM),
                         v_sh.reshape(B, S, DIFF_HEADS, 2 * DIFF_HEAD_DIM))
    return rms_norm(h, final_norm)
```

```python
import math
import numpy as np
import concourse.bass as bass
import concourse.mybir as mybir
from concourse.bass_utils import run_bass_kernel_spmd

F32 = mybir.dt.float32
BF16 = mybir.dt.bfloat16
U8 = mybir.dt.uint8
AF = mybir.ActivationFunctionType
ALU = mybir.AluOpType
AX = mybir.AxisListType

D = 4096
FF = 6144
NCH = D // 128
FCH = FF // 128
TT = 512
EPS = 1e-6
SCALE = 1.0 / math.sqrt(128.0)
NSLOT = 6
RING = 8
N_CORES = 8


class Prog:
    ENGS = ("pe", "act", "dve", "pool", "sp")
    DMA = ("pool", "sp")

    def __init__(self):
        self.ops = {e: [] for e in self.ENGS}
        self.cnt = {e: 0 for e in self.ENGS}
        self.last_w = {}
        self.readers = {}
        self.waited = {e: {} for e in self.ENGS}

    def _need(self, eng, p, idx, waits):
        if p in self.DMA:
            i0 = idx - 1
            key = (p, i0 % RING)
            val = 16 * (i0 // RING + 1)
        else:
            if p == "pe" and eng == "pe":
                return
            key = p
            val = idx
        if self.waited[eng].get(key, 0) >= val:
            return
        self.waited[eng][key] = val
        waits.append((key, val))

    def add(self, eng, fn, reads=(), writes=()):
        n = self.cnt[eng] + 1
        self.cnt[eng] = n
        deps = set()
        lw = self.last_w
        rd = self.readers
        for k in reads:
            w = lw.get(k)
            if w is not None:
                deps.add(w)
        for k in writes:
            w = lw.get(k)
            if w is not None:
                deps.add(w)
            r = rd.get(k)
            if r:
                for e, i in r.items():
                    deps.add((e, i))
        waits = []
        if eng in self.DMA and n - 1 >= RING:
            self._need(eng, eng, n - RING, waits)
        for (p, idx) in sorted(deps):
            if p == eng and idx == n:
                continue
            self._need(eng, p, idx, waits)
        for k in reads:
            r = rd.get(k)
            if r is None:
                rd[k] = {eng: n}
            else:
                r[eng] = n
        for k in writes:
            lw[k] = (eng, n)
            rd[k] = {}
        self.ops[eng].append((waits, fn))
        return (eng, n)

    def barrier(self):
        snap = dict(self.cnt)
        for eng in self.ENGS:
            n = self.cnt[eng] + 1
            self.cnt[eng] = n
            waits = []
            for p in self.ENGS:
                if p == eng and p not in self.DMA:
                    continue
                c = snap[p]
                if c == 0:
                    continue
                if p in self.DMA:
                    for idx in range(max(1, c - RING + 1), c + 1):
                        self._need(eng, p, idx, waits)
                else:
                    self._need(eng, p, c, waits)
            if eng in self.DMA:
                self.ops[eng].append((waits, None))
                self.cnt[eng] = n - 1
            else:
                self.ops[eng].append((waits, "nop"))

    def emit(self, nc, block, sems):
        engmap = {"pe": block.tensor, "act": block.scalar, "dve": block.vector,
                  "pool": block.gpsimd, "sp": block.sync}

        def semof(key):
            return sems[key]

        def make(eng):
            ops = self.ops[eng]
            isdma = eng in self.DMA

            def body(e):
                k = 0
                for waits, fn in ops:
                    for key, val in waits:
                        e.wait_ge(semof(key), val)
                    if fn is None:
                        continue
                    if fn == "nop":
                        ins = e.nop()
                    else:
                        ins = fn(e)
                    if isdma:
                        ins.then_inc(semof((eng, k % RING)), 16)
                    else:
                        ins.then_inc(semof(eng), 1)
                    k += 1
            return body

        for eng in self.ENGS:
            engmap[eng](make(eng))


def lam_init_of(l):
    return 0.8 - 0.6 * math.exp(-0.3 * l)


def build_program(NT, stop_after=None, debug=False):
    T = NT * TT
    NB = T // 128
    nc = bass.Bass("TRN2", target_bir_lowering=False)
    P = Prog()

    def din(name, shape, dt=F32):
        return nc.dram_tensor(name, list(shape), dt, kind="ExternalInput").ap()

    x_d = din("x", [T, D])
    w_ffn_in = din("w_ffn_in", [4, 2, D, 2 * FF])
    w_ffn_out = din("w_ffn_out", [4, 2, FF, D])
    w_qkv_a = din("w_qkv_a", [2, D, 3 * D])
    w_o_a = din("w_o_a", [2, D, D])
    w_kv_b = din("w_kv_b", [D, 2 * D])
    w_q_b = din("w_q_b", [2, D, D])
    w_o_b = din("w_o_b", [2, D, D])
    gains_d = din("gains", [128, 14 * NCH])
    gfin_d = din("gfin", [128, D])
    gsub_d = din("gsub", [128, 4])
    lamrep_d = din("lamrep", [128, 2 * 4 * 128])
    cf32_d = din("cf32", [128, 3 * 128])
    cmask_d = din("cmask", [128, 13 * 512])
    out_d = nc.dram_tensor("out", [T, D], F32, kind="ExternalOutput").ap()

    def dscr(name, shape, dt):
        return nc.dram_tensor(name, list(shape), dt, kind="Internal").ap()

    h_d = dscr("h_scr", [T, D], F32)
    qT_d = dscr("qT_scr", [D, T], BF16)
    kT_d = dscr("kT_scr", [D, T], BF16)
    v_d = dscr("v_scr", [T, D], BF16)
    kshT_d = dscr("kshT_scr", [D, T], BF16)
    vsh_d = dscr("vsh_scr", [T, D], BF16)
    oT_d = dscr("oT_scr", [D, T], BF16)

    ARENA = 206 * 1024
    arena = nc.alloc_sbuf_tensor("arena", [128, ARENA], U8)
    psum = nc.alloc_psum_tensor("psum", [128, 8, 512], F32)

    def view(off, nbytes, dt, **re):
        v = arena[:, off:off + nbytes].bitcast(dt)
        return v

    def v3(off, a, b, dt):
        sz = 4 if dt == F32 else 2
        v = arena[:, off:off + a * b * sz].bitcast(dt)
        return v.rearrange("p (a b) -> p a b", a=a)

    def v2(off, n, dt):
        sz = 4 if dt == F32 else 2
        return arena[:, off:off + n * sz].bitcast(dt)

    o = 0
    H_OFF = o; o += 4 * D * 4
    XN_OFF = o; o += NCH * TT * 2
    AC_OFF = o; o += FCH * TT * 2
    CHAIN_END = o
    WR_OFF = o; o += NSLOT * 4096
    SIL_OFF = o; o += 8192
    YS_OFF = o; o += 4096
    STG_OFF = o; o += 8192
    CF_OFF = o; o += 3 * 128 * 4
    ONB_OFF = o; o += 128 * 2
    GN_OFF = o; o += 14 * NCH * 4
    GSUB_OFF = o; o += 32
    SS_OFF = o; o += 64 * 4
    LAM_OFF = o; o += 64
    JUNK_OFF = o; o += 2048
    assert o <= ARENA, o

    h_sb = v3(H_OFF, 4, D, F32)
    xnT = v3(XN_OFF, NCH, TT, BF16)
    actT = v3(AC_OFF, FCH, TT, BF16)
    wring = [v3(WR_OFF + s * 4096, 4, 512, BF16) for s in range(NSLOT)]
    sil = v3(SIL_OFF, 4, 512, F32)
    ys = v3(YS_OFF, 2, 512, F32)
    stg = [v3(STG_OFF + s * 4096, 4, 512, BF16) for s in range(2)]
    cf = v3(CF_OFF, 3, 128, F32)
    ident, Umat, ones32 = cf[:, 0, :], cf[:, 1, :], cf[:, 2, :]
    onesb = v2(ONB_OFF, 128, BF16)
    gn = v2(GN_OFF, 14 * NCH, F32)
    gsub = v2(GSUB_OFF, 8, F32)
    stat = v2(SS_OFF, 64, F32)
    lam_sb = v2(LAM_OFF, 16, F32)
    junk = v2(JUNK_OFF, 512, F32)
    gfin = v2(XN_OFF, D, F32)

    state = {"wslot": 0, "stg": 0, "psset": 0}

    def ACT(out, in_, func, reads, writes, **kw):
        P.add("act", lambda e: e.activation(out=out, in_=in_, func=func, **kw), reads=reads, writes=writes)

    def STT(out, in0, scalar, in1, op0, op1, reads, writes):
        P.add("dve", lambda e: e.scalar_tensor_tensor(out=out, in0=in0, scalar=scalar, in1=in1, op0=op0, op1=op1),
              reads=reads, writes=writes)

    def TTOP(out, in0, in1, op, reads, writes):
        P.add("dve", lambda e: e.tensor_tensor(out=out, in0=in0, in1=in1, op=op), reads=reads, writes=writes)

    def TS(out, in0, s1, s2, op0, op1, reads, writes):
        if s2 is None:
            P.add("dve", lambda e: e.tensor_scalar(out=out, in0=in0, scalar1=s1, scalar2=None, op0=op0),
                  reads=reads, writes=writes)
        else:
            P.add("dve", lambda e: e.tensor_scalar(out=out, in0=in0, scalar1=s1, scalar2=s2, op0=op0, op1=op1),
                  reads=reads, writes=writes)

    def RECIP(out, in_, reads, writes):
        P.add("dve", lambda e: e.reciprocal(out=out, in_=in_), reads=reads, writes=writes)

    def COPY(out, in_, reads, writes):
        P.add("dve", lambda e: e.tensor_copy(out=out, in_=in_), reads=reads, writes=writes)

    def MM(out, lhsT, rhs, start, stop, reads, writes):
        P.add("pe", lambda e: e.matmul(out, lhsT, rhs, start=start, stop=stop), reads=reads, writes=writes)

    def MMS(lst, reads, writes):
        def fn(e):
            ins = None
            for (o_, l_, r_, st_, sp_) in lst:
                ins = e.matmul(o_, l_, r_, start=st_, stop=sp_)
            return ins
        P.add("pe", fn, reads=reads, writes=writes)

    def DMA(q, out, in_, reads=(), writes=()):
        P.add(q, lambda e: e.dma_start(out=out, in_=in_), reads=reads, writes=writes)

    def wload(src_ap):
        s = state["wslot"]; state["wslot"] = (s + 1) % NSLOT
        dst = wring[s]
        src = src_ap.rearrange("(c p) n -> p c n", p=128)
        P.add("pool", lambda e, dst=dst, src=src: e.dma_start(out=dst, in_=src), writes=[("w", s)])
        return s

    def next_banks():
        b = state["psset"]; state["psset"] = 1 - b
        return b * 4

    def gemm_feat(src, srckey, nch, W, col0, ncols, evac):
        for g in range(ncols // 512):
            b0 = next_banks()
            c0 = col0 + g * 512
            for cq in range(nch // 4):
                s = wload(W[cq * 512:(cq + 1) * 512, c0:c0 + 512])

                def fn(e, s=s, cq=cq, b0=b0):
                    ins = None
                    for ci in range(4):
                        c = cq * 4 + ci
                        for jj in range(4):
                            ins = e.matmul(psum[:, b0 + jj, :], wring[s][:, ci, jj * 128:(jj + 1) * 128],
                                           src[:, c, :], start=(c == 0), stop=(c == nch - 1))
                    return ins
                P.add("pe", fn, reads=[("w", s), srckey], writes=[("ps", b0 + j) for j in range(4)])
            evac(g, b0)

    def gemm_tok(src, srckey, nch, W, col0, ncols, evac):
        for g in range(ncols // 512):
            b0 = next_banks()
            c0 = col0 + g * 512
            for cq in range(nch // 4):
                s = wload(W[cq * 512:(cq + 1) * 512, c0:c0 + 512])

                def fn(e, s=s, cq=cq, b0=b0):
                    ins = None
                    for ci in range(4):
                        c = cq * 4 + ci
                        for st in range(4):
                            ins = e.matmul(psum[:, b0 + st, :], src[:, c, st * 128:(st + 1) * 128],
                                           wring[s][:, ci, :], start=(c == 0), stop=(c == nch - 1))
                    return ins
                P.add("pe", fn, reads=[("w", s), srckey], writes=[("ps", b0 + j) for j in range(4)])
            evac(g, b0)

    def pskeys(b0, n=4):
        return [("ps", b0 + j) for j in range(n)]

    def evac_residual(alpha):
        def ev(g, b0):
            hv = h_sb[:, :, g * 512:(g + 1) * 512]
            P.add("dve", lambda e: e.scalar_tensor_tensor(out=hv, in0=psum[:, b0:b0 + 4, :], scalar=float(alpha),
                                                          in1=hv, op0=ALU.mult, op1=ALU.add),
                  reads=pskeys(b0) + [("h", g)], writes=[("h", g)])
        return ev

    evac_toggle = {"i": 0}

    def evac_store_feat(dstT, tile):
        def ev(g, b0):
            si = state["stg"]; state["stg"] = 1 - si
            sb = stg[si]
            eng = "act" if (evac_toggle["i"] % 2 == 0) else "dve"
            evac_toggle["i"] += 1
            if eng == "act":
                P.add("act", lambda e: e.activation(out=sb, in_=psum[:, b0:b0 + 4, :], func=AF.Copy),
                      reads=pskeys(b0), writes=[("stg", si)])
            else:
                P.add("dve", lambda e: e.tensor_copy(out=sb, in_=psum[:, b0:b0 + 4, :]),
                      reads=pskeys(b0), writes=[("stg", si)])
            dst = dstT[g * 512:(g + 1) * 512, tile * TT:(tile + 1) * TT].rearrange("(j p) t -> p j t", p=128)
            P.add("sp", lambda e: e.dma_start(out=dst, in_=sb), reads=[("stg", si)], writes=[("dram", dstT.name, g, tile)])
        return ev

    def evac_store_tok(dst, tile):
        def ev(g, b0):
            si = state["stg"]; state["stg"] = 1 - si
            sb = stg[si]
            eng = "act" if (evac_toggle["i"] % 2 == 0) else "dve"
            evac_toggle["i"] += 1
            if eng == "act":
                P.add("act", lambda e: e.activation(out=sb, in_=psum[:, b0:b0 + 4, :], func=AF.Copy),
                      reads=pskeys(b0), writes=[("stg", si)])
            else:
                P.add("dve", lambda e: e.tensor_copy(out=sb, in_=psum[:, b0:b0 + 4, :]),
                      reads=pskeys(b0), writes=[("stg", si)])
            d = dst[tile * TT:(tile + 1) * TT, g * 512:(g + 1) * 512].rearrange("(s p) c -> p s c", p=128)
            P.add("sp", lambda e: e.dma_start(out=d, in_=sb), reads=[("stg", si)], writes=[("dram", dst.name, g, tile)])
        return ev

    HKEYS = [("h", g) for g in range(8)]

    def norm_stage(gidx, final=False):
        for s in range(4):
            for g in range(8):
                col = s * 8 + g
                P.add("act", lambda e, s=s, g=g, col=col: e.activation(
                    out=junk, in_=h_sb[:, s, g * 512:(g + 1) * 512], func=AF.Square,
                    accum_out=stat[:, col:col + 1]),
                    reads=[("h", g)], writes=[("junk",), ("stat", col)])
        P.add("dve", lambda e: e.tensor_reduce(out=stat[:, 32:36], in_=stat[:, 0:32].rearrange("p (s g) -> p s g", s=4),
                                               axis=AX.X, op=ALU.add),
              reads=[("stat", c) for c in range(32)], writes=[("stat", "ss")])
        P.add("dve", lambda e: e.tensor_scalar(out=stat[:, 36:40], in0=stat[:, 32:36], scalar1=1.0 / D, scalar2=EPS,
                                               op0=ALU.mult, op1=ALU.add),
              reads=[("stat", "ss")], writes=[("stat", "ms")])
        P.add("act", lambda e: e.activation(out=stat[:, 40:44], in_=stat[:, 36:40], func=AF.Sqrt),
              reads=[("stat", "ms")], writes=[("stat", "sd")])
        P.add("dve", lambda e: e.reciprocal(out=stat[:, 44:48], in_=stat[:, 40:44]),
              reads=[("stat", "sd")], writes=[("stat", "rstd")])
        if final:
            for s in range(4):
                P.add("dve", lambda e, s=s: e.scalar_tensor_tensor(
                    out=h_sb[:, s, :], in0=h_sb[:, s, :], scalar=stat[:, 44 + s:45 + s], in1=gfin,
                    op0=ALU.mult, op1=ALU.mult),
                    reads=HKEYS + [("stat", "rstd"), ("gfin",)], writes=HKEYS)
            return
        k = 0
        for s in range(4):
            for g in range(8):
                yi = k % 2
                bank = 0 if (k % 2 == 0) else 4
                k += 1
                P.add("act", lambda e, s=s, g=g, yi=yi: e.activation(
                    out=ys[:, yi, :], in_=h_sb[:, s, g * 512:(g + 1) * 512], func=AF.Copy,
                    scale=stat[:, 44 + s:45 + s]),
                    reads=[("h", g), ("stat", "rstd")], writes=[("ys", yi)])

                def tfn(e, yi=yi, bank=bank):
                    ins = None
                    for jj in range(4):
                        ins = e.transpose(out=psum[:, bank, jj * 128:(jj + 1) * 128],
                                          in_=ys[:, yi, jj * 128:(jj + 1) * 128], identity=ident)
                    return ins
                P.add("pe", tfn, reads=[("ys", yi), ("cf",)], writes=[("ps", bank)])
                for jj in range(4):
                    c = g * 4 + jj
                    P.add("dve", lambda e, c=c, s=s, jj=jj, bank=bank: e.tensor_scalar(
                        out=xnT[:, c, s * 128:(s + 1) * 128], in0=psum[:, bank, jj * 128:(jj + 1) * 128],
                        scalar1=gn[:, gidx * NCH + c:gidx * NCH + c + 1], scalar2=None, op0=ALU.mult),
                        reads=[("ps", bank), ("gn",)], writes=[("xnT",)])

    def ffn_half(l, i):
        norm_stage(l * 2 + i)
        Win = w_ffn_in[l, i]
        Wout = w_ffn_out[l, i]
        for jg in range(FCH // 4):
            def ev_gate(g, b0):
                P.add("act", lambda e: e.activation(out=sil, in_=psum[:, b0:b0 + 4, :], func=AF.Silu),
                      reads=pskeys(b0), writes=[("sil",)])
            gemm_feat(xnT, ("xnT",), NCH, Win, jg * 512, 512, ev_gate)

            def ev_up(g, b0, jg=jg):
                P.add("dve", lambda e: e.tensor_tensor(out=actT[:, jg * 4:(jg + 1) * 4, :], in0=psum[:, b0:b0 + 4, :],
                                                       in1=sil, op=ALU.mult),
                      reads=pskeys(b0) + [("sil",)], writes=[("actT",)])
            gemm_feat(xnT, ("xnT",), NCH, Win, FF + jg * 512, 512, ev_up)
        gemm_tok(actT, ("actT",), FCH, Wout, 0, D, evac_residual(0.5))

    def load_h(src, tile):
        s_ap = src[tile * TT:(tile + 1) * TT, :].rearrange("(s p) d -> p s d", p=128)
        P.add("sp", lambda e: e.dma_start(out=h_sb, in_=s_ap), reads=[("dram", src.name, "h", tile)], writes=HKEYS)

    def store_h(dst, tile):
        d_ap = dst[tile * TT:(tile + 1) * TT, :].rearrange("(s p) d -> p s d", p=128)
        P.add("sp", lambda e: e.dma_start(out=d_ap, in_=h_sb), reads=HKEYS, writes=[("dram", dst.name, "h", tile)])

    def load_oT(tile):
        s_ap = oT_d[:, tile * TT:(tile + 1) * TT].rearrange("(c p) t -> p c t", p=128)
        P.add("sp", lambda e: e.dma_start(out=xnT, in_=s_ap), reads=[("dram", "oT")], writes=[("xnT",)])

    def oproj(W, tile):
        load_oT(tile)
        gemm_tok(xnT, ("xnT",), NCH, W, 0, D, evac_residual(1.0))

    P.add("sp", lambda e: e.dma_start(out=arena[:, CF_OFF:CF_OFF + 3 * 128 * 4].bitcast(F32), in_=cf32_d), writes=[("cf",)])
    P.add("sp", lambda e: e.dma_start(out=gn, in_=gains_d), writes=[("gn",)])
    P.add("sp", lambda e: e.dma_start(out=gsub[:, 0:4], in_=gsub_d), writes=[("gsub",)])
    P.add("dve", lambda e: e.memset(onesb, 1.0), writes=[("onesb",)])

    a = 0
    AQ_OFF = a; a += 2 * 2 * T * 2
    AK_OFF = a; a += 2 * 2 * T * 2
    AV_OFF = a; a += 2 * NB * 256 * 2
    AO_OFF = a; a += 2 * 2 * T * 2
    MSK_OFF = a; a += 8 * 512 * 4
    TMP_OFF = a; a += 4 * 3 * 2048
    AB_OFF = a; a += 4 * 1024
    SSUM_OFF = a; a += 4 * 2048
    R_OFF = a; a += 3 * 2048
    OF_OFF = a; a += 2 * 2048
    SQ_OFF = a; a += 2 * 2048
    assert a <= CHAIN_END, a

    aq = [v3(AQ_OFF + b * 2 * T * 2, 2, T, BF16) for b in range(2)]
    ak = [v3(AK_OFF + b * 2 * T * 2, 2, T, BF16) for b in range(2)]
    av = [v3(AV_OFF + b * NB * 256 * 2, NB, 256, BF16) for b in range(2)]
    ao = [v3(AO_OFF + b * 2 * T * 2, 2, T, BF16) for b in range(2)]
    msk = v3(MSK_OFF, 8, 512, F32)
    tmp = [[v2(TMP_OFF + (p * 3 + i) * 2048, 512, F32) for i in range(3)] for p in range(4)]
    abf = [v2(AB_OFF + p * 1024, 512, BF16) for p in range(4)]
    ssums = [v2(SSUM_OFF + p * 2048, 512, F32) for p in range(4)]
    rbuf = [v2(R_OFF + i * 2048, 512, F32) for i in range(3)]
    ofp = v3(OF_OFF, 2, 512, F32)
    sqf = v3(SQ_OFF, 2, 512, F32)

    def load_masks(which):
        if which == "a":
            DMA("sp", arena[:, MSK_OFF:MSK_OFF + 8 * 512 * 4].bitcast(F32), cmask_d[:, 0:8 * 512], writes=[("msk",)])
        else:
            DMA("sp", arena[:, MSK_OFF:MSK_OFF + 5 * 512 * 4].bitcast(F32), cmask_d[:, 8 * 512:13 * 512], writes=[("msk",)])

    def attn_a(l):
        P.barrier()
        load_masks("a")
        H = 32
        NS = 4

        def load_head(h, b):
            DMA("sp", aq[b][:, 0, :], qT_d[h * 128:(h + 1) * 128, :], writes=[("aq", b)])
            DMA("sp", ak[b][:, 0, :], kT_d[h * 128:(h + 1) * 128, :], writes=[("ak", b)])
            DMA("sp", av[b][:, :, 0:128], v_d[:, h * 128:(h + 1) * 128].rearrange("(b p) d -> p b d", p=128),
                writes=[("av", b)])

        def stream(h, qt, slot):
            b = h % 2
            kbs = list(range(4 * qt + 3, -1, -1))
            zc = 2 * slot
            ob = 2 * slot + 1
            E, SPb, Wb = tmp[slot]
            A = abf[slot]
            ssum = ssums[slot]
            zps = psum[:, zc, :]
            qsl = aq[b][:, 0, qt * TT:(qt + 1) * TT]
            kE, kSP, kW, kA, kS = ("E", slot), ("SP", slot), ("W", slot), ("A", slot), ("ssum", slot)
            for idx, kb in enumerate(kbs):
                diag = kb >= 4 * qt
                oi = kb - 4 * qt
                last = idx == len(kbs) - 1
                MM(zps, ak[b][:, 0, kb * 128:(kb + 1) * 128], qsl, True, True,
                   reads=[("ak", b), ("aq", b)], writes=[("ps", zc)])
                yield
                ACT(E, zps, AF.Exp, reads=[("ps", zc)], writes=[kE], scale=SCALE)
                yield
                ACT(SPb, E, AF.Ln, reads=[kE], writes=[kSP], bias=1.0)
                yield
                STT(Wb, zps, SCALE, SPb, ALU.mult, ALU.subtract, reads=[("ps", zc), kSP], writes=[kW])
                if diag:
                    TTOP(SPb, SPb, msk[:, oi, :], ALU.mult, reads=[kSP, ("msk",)], writes=[kSP])
                    TTOP(Wb, Wb, msk[:, 4 + oi, :], ALU.add, reads=[kW, ("msk",)], writes=[kW])
                yield
                if idx == 0:
                    MM(zps, Umat, SPb, True, True, reads=[kSP, ("cf",)], writes=[("ps", zc)])
                else:
                    MMS([(zps, Umat, SPb, True, False), (zps, ones32, ssum, False, True)],
                        reads=[kSP, kS, ("cf",)], writes=[("ps", zc)])
                yield
                TTOP(Wb, Wb, zps, ALU.subtract, reads=[kW, ("ps", zc)], writes=[kW])
                if not last:
                    if idx == 0:
                        COPY(ssum, SPb, reads=[kSP], writes=[kS])
                    else:
                        TTOP(ssum, ssum, SPb, ALU.add, reads=[kSP, kS], writes=[kS])
                yield
                ACT(A, Wb, AF.Exp, reads=[kW], writes=[kA])
                yield
                MM(psum[:, ob, :], av[b][:, kb, 0:128], A, idx == 0, last,
                   reads=[("av", b), kA], writes=[("ps", ob)])
                yield
            ACT(ao[b][:, 0, qt * TT:(qt + 1) * TT], psum[:, ob, :], AF.Copy, reads=[("ps", ob)], writes=[("ao", b, qt)])
            yield

        queue = [(h, qt) for h in range(H) for qt in range(NT)]
        queue.reverse()
        active = [None] * NS
        meta = [None] * NS
        done_cnt = {}
        load_head(0, 0)
        load_head(1, 1)
        loaded = {0, 1}
        while queue or any(a_ is not None for a_ in active):
            for slot in range(NS):
                if active[slot] is None and queue and queue[-1][0] in loaded:
                    h, qt = queue.pop()
                    active[slot] = stream(h, qt, slot)
                    meta[slot] = h
                if active[slot] is not None:
                    try:
                        next(active[slot])
                    except StopIteration:
                        h = meta[slot]
                        active[slot] = None
                        done_cnt[h] = done_cnt.get(h, 0) + 1
                        if done_cnt[h] == NT:
                            b = h % 2
                            DMA("sp", oT_d[h * 128:(h + 1) * 128, :], ao[b][:, 0, :],
                                reads=[("ao", b, q_) for q_ in range(NT)], writes=[("dram", "oT")])
                            if h + 2 < H:
                                load_head(h + 2, b)
                                loaded.add(h + 2)
        P.barrier()

    def attn_b(l):
        j = l - 2
        lam_init = lam_init_of(l)
        P.barrier()
        load_masks("b")
        lrep = v3(TMP_OFF, 4, 128, F32)
        LK = ("U", 0)
        DMA("sp", lrep, lamrep_d[:, j * 512:(j + 1) * 512].rearrange("p (w i) -> p w i", w=4), writes=[LK])
        TTOP(lrep[:, 0, :], lrep[:, 0, :], lrep[:, 1, :], ALU.mult, reads=[LK], writes=[LK])
        TTOP(lrep[:, 2, :], lrep[:, 2, :], lrep[:, 3, :], ALU.mult, reads=[LK], writes=[LK])
        P.add("dve", lambda e: e.tensor_reduce(out=lam_sb[:, 0:1], in_=lrep[:, 0, :], axis=AX.X, op=ALU.add),
              reads=[LK], writes=[("lam", 0)])
        P.add("dve", lambda e: e.tensor_reduce(out=lam_sb[:, 1:2], in_=lrep[:, 2, :], axis=AX.X, op=ALU.add),
              reads=[LK], writes=[("lam", 1)])
        ACT(lam_sb[:, 2:4], lam_sb[:, 0:2], AF.Exp, reads=[("lam", 0), ("lam", 1)], writes=[("lam", 2)])
        TTOP(lam_sb[:, 4:5], lam_sb[:, 2:3], lam_sb[:, 3:4], ALU.subtract, reads=[("lam", 2)], writes=[("lam", 4)])
        TS(lam_sb[:, 5:6], lam_sb[:, 4:5], float(lam_init), None, ALU.add, None, reads=[("lam", 4)], writes=[("lam", 5)])
        lam_ap = lam_sb[:, 5:6]
        H = 16

        def load_head(h, b):
            DMA("sp", aq[b], qT_d[h * 256:(h + 1) * 256, :].rearrange("(c p) t -> p c t", p=128), writes=[("aq", b)])
            DMA("sp", ak[b], kshT_d[h * 256:(h + 1) * 256, :].rearrange("(c p) t -> p c t", p=128), writes=[("ak", b)])
            DMA("sp", av[b], vsh_d[:, h * 256:(h + 1) * 256].rearrange("(b p) d -> p b d", p=128), writes=[("av", b)])
        load_head(0, 0)
        r1, r2l, rstd = rbuf
        ucnt = 0
        for h in range(H):
            b = h % 2
            slope = 2.0 ** (-8.0 * (h + 1) / 16.0)
            ch = -slope / SCALE
            if h + 1 < H:
                load_head(h + 1, 1 - b)
            for qt in range(NT):
                kbs = list(range(4 * qt + 3, -1, -1))
                units = [(idx, kb, c) for idx, kb in enumerate(kbs) for c in range(2)]
                pend = []

                def front(idx, kb, c, ui):
                    diag = kb >= 4 * qt
                    oi = kb - 4 * qt
                    delta = qt * TT - kb * 128
                    zb = c
                    Ub = tmp[ui % 4][0]
                    Eb = abf[ui % 4]
                    zps = psum[:, zb, :]
                    MM(zps, ak[b][:, c, kb * 128:(kb + 1) * 128], aq[b][:, c, qt * TT:(qt + 1) * TT], True, True,
                       reads=[("ak", b), ("aq", b)], writes=[("ps", zb)])
                    mt = msk[:, 1 + oi, :] if diag else msk[:, 0, :]
                    STT(Ub, mt, float(ch), zps, ALU.mult, ALU.add, reads=[("ps", zb), ("msk",)], writes=[("U", ui % 4)])
                    bconst = 0.0 if diag else float(-slope * delta)
                    ACT(Eb, Ub, AF.Exp, reads=[("U", ui % 4)], writes=[("Eb", ui % 4)], scale=SCALE, bias=bconst)

                def back(idx, kb, c, ui):
                    last = idx == len(kbs) - 1
                    Eb = abf[ui % 4]
                    MMS([(psum[:, 2 + 2 * c, :], av[b][:, kb, 0:128], Eb, idx == 0, last),
                         (psum[:, 3 + 2 * c, :], av[b][:, kb, 128:256], Eb, idx == 0, last),
                         (psum[:, 6 + c, :], onesb, Eb, idx == 0, last)],
                        reads=[("av", b), ("Eb", ui % 4), ("onesb",)],
                        writes=[("ps", 2 + 2 * c), ("ps", 3 + 2 * c), ("ps", 6 + c)])

                LAG = 1
                n = len(units)
                uis = []
                for i in range(n + LAG):
                    if i < n:
                        uis.append(ucnt)
                        front(*units[i], ucnt)
                        ucnt += 1
                    if i >= LAG:
                        back(*units[i - LAG], uis[i - LAG])
                RECIP(r1, psum[:, 6, :], reads=[("ps", 6)], writes=[("r", 0)])
                RECIP(r2l, psum[:, 7, :], reads=[("ps", 7)], writes=[("r", 1)])
                TS(r2l, r2l, lam_ap, None, ALU.mult, None, reads=[("r", 1), ("lam", 5)], writes=[("r", 1)])
                for half in range(2):
                    TTOP(ofp[:, half, :], psum[:, 2 + half, :], r1, ALU.mult, reads=[("ps", 2 + half), ("r", 0)],
                         writes=[("of", half)])
                    TTOP(sqf[:, half, :], psum[:, 4 + half, :], r2l, ALU.mult, reads=[("ps", 4 + half), ("r", 1)],
                         writes=[("sq", half)])
                    TTOP(ofp[:, half, :], ofp[:, half, :], sqf[:, half, :], ALU.subtract,
                         reads=[("of", half), ("sq", half)], writes=[("of", half)])
                ACT(sqf, ofp, AF.Square, reads=[("of", 0), ("of", 1), ("sq", 0), ("sq", 1)], writes=[("sq", 0), ("sq", 1)])
                MMS([(psum[:, 0, :], ones32, sqf[:, 0, :], True, False), (psum[:, 0, :], ones32, sqf[:, 1, :], False, True)],
                    reads=[("sq", 0), ("sq", 1), ("cf",)], writes=[("ps", 0)])
                TS(rstd, psum[:, 0, :], 1.0 / 256.0, EPS, ALU.mult, ALU.add, reads=[("ps", 0)], writes=[("r", 2)])
                ACT(rstd, rstd, AF.Sqrt, reads=[("r", 2)], writes=[("r", 2)])
                RECIP(rstd, rstd, reads=[("r", 2)], writes=[("r", 2)])
                TS(rstd, rstd, float(1.0 - lam_init), None, ALU.mult, None, reads=[("r", 2)], writes=[("r", 2)])
                for half in range(2):
                    STT(ao[b][:, half, qt * TT:(qt + 1) * TT], ofp[:, half, :], gsub[:, j * 2 + half:j * 2 + half + 1], rstd,
                        ALU.mult, ALU.mult, reads=[("of", half), ("r", 2), ("gsub",)], writes=[("ao", b)])
            DMA("sp", oT_d[h * 256:(h + 1) * 256, :].rearrange("(c p) t -> p c t", p=128), ao[b], reads=[("ao", b)],
                writes=[("dram", "oT")])
        P.barrier()

    def stop(name):
        return stop_after == name

    dbg_names = []

    def dump(name, tile):
        if not debug:
            return
        if name not in dbg_names:
            dbg_names.append(name)
        dd = nc.dram_tensor("dbg_" + name, [T, D], F32, kind="ExternalOutput").ap() if tile == 0 else dbg_aps[name]
        dbg_aps[name] = dd
        DMA("sp", dd[tile * TT:(tile + 1) * TT, :].rearrange("(s p) d -> p s d", p=128), h_sb, reads=HKEYS,
            writes=[("dram", "dbg", name, tile)])

    dbg_aps = {}

    def finish(tile):
        P.add("sp", lambda e: e.dma_start(out=out_d[tile * TT:(tile + 1) * TT, :].rearrange("(s p) d -> p s d", p=128),
                                          in_=h_sb),
              reads=HKEYS, writes=[("dram", "out", tile)])

    def qkv_a(l, tile):
        W = w_qkv_a[l]
        gemm_feat(xnT, ("xnT",), NCH, W, 0, D, evac_store_feat(qT_d, tile))
        gemm_feat(xnT, ("xnT",), NCH, W, D, D, evac_store_feat(kT_d, tile))
        gemm_tok(xnT, ("xnT",), NCH, W, 2 * D, D, evac_store_tok(v_d, tile))

    def dram_sync(names):
        pass

    def run():
        for tile in range(NT):
            load_h(x_d, tile)
            ffn_half(0, 0)
            dump("ffn00", tile)
            if stop("ffn00"):
                finish(tile); continue
            norm_stage(8 + 0)
            qkv_a(0, tile)
            store_h(h_d, tile)
        if stop("ffn00"):
            return
        attn_a(0)
        for tile in range(NT):
            load_h(h_d, tile)
            oproj(w_o_a[0], tile)
            dump("attn0", tile)
            if stop("attn0"):
                finish(tile); continue
            ffn_half(0, 1)
            dump("ffn01", tile)
            ffn_half(1, 0)
            dump("ffn10", tile)
            norm_stage(8 + 1)
            qkv_a(1, tile)
            store_h(h_d, tile)
        if stop("attn0"):
            return
        attn_a(1)
        for tile in range(NT):
            load_h(h_d, tile)
            oproj(w_o_a[1], tile)
            dump("attn1", tile)
            ffn_half(1, 1)
            dump("ffn11", tile)
            if stop("ffn11"):
                finish(tile); continue
            norm_stage(12)
            gemm_feat(xnT, ("xnT",), NCH, w_kv_b, 0, D, evac_store_feat(kshT_d, tile))
            gemm_tok(xnT, ("xnT",), NCH, w_kv_b, D, D, evac_store_tok(vsh_d, tile))
            ffn_half(2, 0)
            dump("ffn20", tile)
            norm_stage(8 + 2)
            gemm_feat(xnT, ("xnT",), NCH, w_q_b[0], 0, D, evac_store_feat(qT_d, tile))
            store_h(h_d, tile)
        if stop("ffn11"):
            return
        attn_b(2)
        for tile in range(NT):
            load_h(h_d, tile)
            oproj(w_o_b[0], tile)
            dump("attn2", tile)
            if stop("attn2"):
                finish(tile); continue
            ffn_half(2, 1)
            dump("ffn21", tile)
            ffn_half(3, 0)
            dump("ffn30", tile)
            norm_stage(8 + 3)
            gemm_feat(xnT, ("xnT",), NCH, w_q_b[1], 0, D, evac_store_feat(qT_d, tile))
            store_h(h_d, tile)
        if stop("attn2"):
            return
        attn_b(3)
        for tile in range(NT):
            load_h(h_d, tile)
            oproj(w_o_b[1], tile)
            dump("attn3", tile)
            ffn_half(3, 1)
            dump("ffn31", tile)
            P.add("sp", lambda e: e.dma_start(out=gfin, in_=gfin_d), reads=[("xnT",)], writes=[("gfin",), ("xnT",)])
            norm_stage(13, final=True)
            finish(tile)

    run()
    P.barrier()

    sems = {}
    import contextlib
    with contextlib.ExitStack() as es:
        for e in ("pe", "act", "dve"):
            sems[e] = es.enter_context(nc.semaphore("s_" + e))
        for q in Prog.DMA:
            for r in range(RING):
                sems[(q, r)] = es.enter_context(nc.semaphore(f"s_{q}{r}"))
        block = es.enter_context(nc.Block())
        P.emit(nc, block, sems)
    return nc, P


def make_consts():
    p = np.arange(128)[:, None]
    f = np.arange(512)[None, :]
    ident = np.eye(128, dtype=np.float32)
    U = (np.arange(128)[:, None] > np.arange(128)[None, :]).astype(np.float32)
    ones = np.ones((128, 128), np.float32)
    cf32 = np.concatenate([ident, U, ones], axis=1)
    tiles = []
    for oi in range(4):
        tiles.append((f > p + oi * 128).astype(np.float32))
    for oi in range(4):
        tiles.append(np.where(f > p + oi * 128, 0.0, -30000.0).astype(np.float32))
    tiles.append((f - p).astype(np.float32) + np.zeros((128, 512), np.float32))
    for oi in range(4):
        s = p + oi * 128
        allowed = (s // 64) <= (f // 64)
        tiles.append(np.where(allowed, np.abs(f - s), 1.0e6).astype(np.float32))
    cmask = np.concatenate(tiles, axis=1)
    return np.ascontiguousarray(cf32), np.ascontiguousarray(cmask)


def layout_small(inputs):
    g_all = np.concatenate([
        np.asarray(inputs["ffn_norm"], np.float32).reshape(8, D),
        np.asarray(inputs["attn_norm"], np.float32).reshape(4, D),
        np.asarray(inputs["kv_norm"], np.float32).reshape(1, D),
        np.asarray(inputs["final_norm"], np.float32).reshape(1, D)], axis=0)
    gains = np.ascontiguousarray(g_all.reshape(14, NCH, 128).transpose(2, 0, 1).reshape(128, 14 * NCH))
    gfin = np.ascontiguousarray(np.broadcast_to(np.asarray(inputs["final_norm"], np.float32)[None, :], (128, D)))
    gsub = np.ascontiguousarray(np.asarray(inputs["subln_norm"], np.float32).reshape(2, 2, 128).transpose(2, 0, 1).reshape(128, 4))
    lam = np.stack([np.asarray(inputs[k], np.float32) for k in ("lambda_q1", "lambda_k1", "lambda_q2", "lambda_k2")], axis=1)
    lamrep = np.ascontiguousarray(np.broadcast_to(lam.reshape(1, 2 * 4 * 128), (128, 2 * 4 * 128)))
    return gains, gfin, gsub, lamrep


_CACHE = {}


def run_cores(inputs, NT, n_cores, stop_after=None, trace=False, debug=False):
    key = (NT, stop_after, debug)
    if key not in _CACHE:
        _CACHE[key] = build_program(NT, stop_after, debug)
    nc, _ = _CACHE[key]
    T = NT * TT
    gains, gfin, gsub, lamrep = layout_small(inputs)
    cf32, cmask = make_consts()
    x = np.asarray(inputs["x"], np.float32)
    shared = {
        "w_ffn_in": np.asarray(inputs["w_ffn_in"], np.float32),
        "w_ffn_out": np.asarray(inputs["w_ffn_out"], np.float32),
        "w_qkv_a": np.asarray(inputs["w_qkv_a"], np.float32),
        "w_o_a": np.asarray(inputs["w_o_a"], np.float32),
        "w_kv_b": np.asarray(inputs["w_kv_b"], np.float32),
        "w_q_b": np.asarray(inputs["w_q_b"], np.float32),
        "w_o_b": np.asarray(inputs["w_o_b"], np.float32),
        "gains": gains, "gfin": gfin, "gsub": gsub, "lamrep": lamrep, "cf32": cf32, "cmask": cmask,
    }
    in_maps = []
    for c in range(n_cores):
        m = dict(shared)
        m["x"] = np.ascontiguousarray(x[c, :T])
        in_maps.append(m)
    res = run_bass_kernel_spmd(nc, in_maps, core_ids=list(range(n_cores)), trace=trace)
    return res


def kernel(**inputs):
    res = run_cores(inputs, NT=4, n_cores=N_CORES)
    out = np.stack([np.asarray(r["out"], np.float32) for r in res.results], axis=0)
    return out
```

```python
import math
import numpy as np
import concourse.bass as bass
import concourse.mybir as mybir
from concourse.bass_utils import run_bass_kernel_spmd

F32 = mybir.dt.float32
BF16 = mybir.dt.bfloat16
U8 = mybir.dt.uint8
AF = mybir.ActivationFunctionType
ALU = mybir.AluOpType
AX = mybir.AxisListType

D = 4096
FF = 6144
NCH = D // 128
FCH = FF // 128
TT = 512
EPS = 1e-6
SCALE = 1.0 / math.sqrt(128.0)
NSLOT = 6
RING = 8
N_CORES = 8


class Prog:
    ENGS = ("pe", "act", "dve", "pc", "pool", "sp")
    DMA = ("pool", "sp")
    HWQ = {"pe": "tensor", "act": "scalar", "dve": "vector", "pc": "gpsimd", "pool": "gpsimd", "sp": "sync"}

    def __init__(self):
        self.seq = 0
        self.ops = {e: [] for e in self.ENGS}
        self.cnt = {e: 0 for e in self.ENGS}
        self.last_w = {}
        self.readers = {}
        self.waited = {e: {} for e in self.ENGS}

    def _need(self, eng, p, idx, waits):
        if p in self.DMA:
            i0 = idx - 1
            key = (p, i0 % RING)
            val = 16 * (i0 // RING + 1)
        else:
            if p == "pe" and eng == "pe":
                return
            key = p
            val = idx
        if self.waited[eng].get(key, 0) >= val:
            return
        self.waited[eng][key] = val
        waits.append((key, val))

    def add(self, eng, fn, reads=(), writes=()):
        n = self.cnt[eng] + 1
        self.cnt[eng] = n
        deps = set()
        lw = self.last_w
        rd = self.readers
        for k in reads:
            w = lw.get(k)
            if w is not None:
                deps.add(w)
        for k in writes:
            w = lw.get(k)
            if w is not None:
                deps.add(w)
            r = rd.get(k)
            if r:
                for e, i in r.items():
                    deps.add((e, i))
        waits = []
        if eng in self.DMA and n - 1 >= RING:
            self._need(eng, eng, n - RING, waits)
        for (p, idx) in sorted(deps):
            if p == eng and idx == n:
                continue
            self._need(eng, p, idx, waits)
        for k in reads:
            r = rd.get(k)
            if r is None:
                rd[k] = {eng: n}
            else:
                r[eng] = n
        for k in writes:
            lw[k] = (eng, n)
            rd[k] = {}
        self.seq += 1
        self.ops[eng].append((self.seq, waits, fn))
        return (eng, n)

    def barrier(self):
        snap = dict(self.cnt)
        for eng in self.ENGS:
            n = self.cnt[eng] + 1
            self.cnt[eng] = n
            waits = []
            for p in self.ENGS:
                if p == eng and p not in self.DMA:
                    continue
                c = snap[p]
                if c == 0:
                    continue
                if p in self.DMA:
                    for idx in range(max(1, c - RING + 1), c + 1):
                        self._need(eng, p, idx, waits)
                else:
                    self._need(eng, p, c, waits)
            self.seq += 1
            if eng in self.DMA:
                self.ops[eng].append((self.seq, waits, None))
                self.cnt[eng] = n - 1
            else:
                self.ops[eng].append((self.seq, waits, "nop"))

    def emit(self, nc, block, sems):
        hw = {"tensor": block.tensor, "scalar": block.scalar, "vector": block.vector,
              "gpsimd": block.gpsimd, "sync": block.sync}

        def make(queue):
            merged = []
            for eng in self.ENGS:
                if self.HWQ[eng] == queue:
                    merged.extend((seq, eng, waits, fn) for (seq, waits, fn) in self.ops[eng])
            merged.sort(key=lambda t: t[0])

            def body(e):
                k = {eng: 0 for eng in self.ENGS}
                for seq, eng, waits, fn in merged:
                    for key, val in waits:
                        e.wait_ge(sems[key], val)
                    if fn is None:
                        continue
                    if fn == "nop":
                        ins = e.nop()
                    else:
                        ins = fn(e)
                    if eng in self.DMA:
                        ins.then_inc(sems[(eng, k[eng] % RING)], 16)
                    else:
                        ins.then_inc(sems[eng], 1)
                    k[eng] += 1
            return body

        for queue in ("tensor", "scalar", "vector", "gpsimd", "sync"):
            hw[queue](make(queue))


def lam_init_of(l):
    return 0.8 - 0.6 * math.exp(-0.3 * l)


def build_program(NT, stop_after=None, debug=False):
    T = NT * TT
    NB = T // 128
    nc = bass.Bass("TRN2", target_bir_lowering=False)
    P = Prog()

    def din(name, shape, dt=F32):
        return nc.dram_tensor(name, list(shape), dt, kind="ExternalInput").ap()

    x_d = din("x", [T, D])
    w_ffn_in = din("w_ffn_in", [4, 2, D, 2 * FF])
    w_ffn_out = din("w_ffn_out", [4, 2, FF, D])
    w_qkv_a = din("w_qkv_a", [2, D, 3 * D])
    w_o_a = din("w_o_a", [2, D, D])
    w_kv_b = din("w_kv_b", [D, 2 * D])
    w_q_b = din("w_q_b", [2, D, D])
    w_o_b = din("w_o_b", [2, D, D])
    gains_d = din("gains", [128, 14 * NCH])
    gfin_d = din("gfin", [128, D])
    gsub_d = din("gsub", [128, 4])
    lamrep_d = din("lamrep", [128, 2 * 4 * 128])
    cf32_d = din("cf32", [128, 3 * 128])
    cmask_d = din("cmask", [128, 13 * 512])
    out_d = nc.dram_tensor("out", [T, D], F32, kind="ExternalOutput").ap()

    def dscr(name, shape, dt):
        return nc.dram_tensor(name, list(shape), dt, kind="Internal").ap()

    h_d = dscr("h_scr", [T, D], F32)
    qT_d = dscr("qT_scr", [D, T], BF16)
    kT_d = dscr("kT_scr", [D, T], BF16)
    v_d = dscr("v_scr", [T, D], BF16)
    kshT_d = dscr("kshT_scr", [D, T], BF16)
    vsh_d = dscr("vsh_scr", [T, D], BF16)
    oT_d = dscr("oT_scr", [D, T], BF16)

    ARENA = 206 * 1024
    arena = nc.alloc_sbuf_tensor("arena", [128, ARENA], U8)
    psum = nc.alloc_psum_tensor("psum", [128, 8, 512], F32)

    def view(off, nbytes, dt, **re):
        v = arena[:, off:off + nbytes].bitcast(dt)
        return v

    def v3(off, a, b, dt):
        sz = 4 if dt == F32 else 2
        v = arena[:, off:off + a * b * sz].bitcast(dt)
        return v.rearrange("p (a b) -> p a b", a=a)

    def v2(off, n, dt):
        sz = 4 if dt == F32 else 2
        return arena[:, off:off + n * sz].bitcast(dt)

    o = 0
    H_OFF = o; o += 4 * D * 4
    XN_OFF = o; o += NCH * TT * 2
    AC_OFF = o; o += FCH * TT * 2
    CHAIN_END = o
    WR_OFF = o; o += NSLOT * 4096
    SIL_OFF = o; o += 8192
    YS_OFF = o; o += 4096
    STG_OFF = o; o += 8192
    CF_OFF = o; o += 3 * 128 * 4
    ONB_OFF = o; o += 128 * 2
    GN_OFF = o; o += 14 * NCH * 4
    GSUB_OFF = o; o += 32
    SS_OFF = o; o += 64 * 4
    LAM_OFF = o; o += 64
    JUNK_OFF = o; o += 2 * 2048
    assert o <= ARENA, o

    h_sb = v3(H_OFF, 4, D, F32)
    xnT = v3(XN_OFF, NCH, TT, BF16)
    actT = v3(AC_OFF, FCH, TT, BF16)
    wring = [v3(WR_OFF + s * 4096, 4, 512, BF16) for s in range(NSLOT)]
    sil = v3(SIL_OFF, 4, 512, F32)
    ys = v3(YS_OFF, 2, 512, F32)
    stg = [v3(STG_OFF + s * 4096, 4, 512, BF16) for s in range(2)]
    cf = v3(CF_OFF, 3, 128, F32)
    ident, Umat, ones32 = cf[:, 0, :], cf[:, 1, :], cf[:, 2, :]
    onesb = v2(ONB_OFF, 128, BF16)
    gn = v2(GN_OFF, 14 * NCH, F32)
    gsub = v2(GSUB_OFF, 8, F32)
    stat = v2(SS_OFF, 64, F32)
    lam_sb = v2(LAM_OFF, 16, F32)
    junks = [v2(JUNK_OFF + i * 2048, 512, F32) for i in range(2)]
    jstate = {"i": 0}

    def square_acc(s, g):
        ji = jstate["i"]; jstate["i"] = 1 - ji
        col = s * 8 + g
        ACT(junks[ji], h_sb[:, s, g * 512:(g + 1) * 512], AF.Square, reads=[("h", g)],
            writes=[("junk", ji), ("stat", col)], accum_out=stat[:, col:col + 1])
    gfin = v2(XN_OFF, D, F32)

    state = {"wslot": 0, "stg": 0, "psset": 0}

    def ACT(out, in_, func, reads, writes, **kw):
        P.add("act", lambda e: e.activation(out=out, in_=in_, func=func, **kw), reads=reads, writes=writes)

    def STT(out, in0, scalar, in1, op0, op1, reads, writes):
        P.add("dve", lambda e: e.scalar_tensor_tensor(out=out, in0=in0, scalar=scalar, in1=in1, op0=op0, op1=op1),
              reads=reads, writes=writes)

    def TTOP(out, in0, in1, op, reads, writes):
        P.add("dve", lambda e: e.tensor_tensor(out=out, in0=in0, in1=in1, op=op), reads=reads, writes=writes)

    def TS(out, in0, s1, s2, op0, op1, reads, writes):
        if s2 is None:
            P.add("dve", lambda e: e.tensor_scalar(out=out, in0=in0, scalar1=s1, scalar2=None, op0=op0),
                  reads=reads, writes=writes)
        else:
            P.add("dve", lambda e: e.tensor_scalar(out=out, in0=in0, scalar1=s1, scalar2=s2, op0=op0, op1=op1),
                  reads=reads, writes=writes)

    def PTT(out, in0, in1, op, reads, writes):
        P.add("pc", lambda e: e.tensor_tensor(out=out, in0=in0, in1=in1, op=op), reads=reads, writes=writes)

    def PCOPY(out, in_, reads, writes):
        P.add("pc", lambda e: e.tensor_copy(out=out, in_=in_), reads=reads, writes=writes)

    def RECIP(out, in_, reads, writes):
        P.add("dve", lambda e: e.reciprocal(out=out, in_=in_), reads=reads, writes=writes)

    def COPY(out, in_, reads, writes):
        P.add("dve", lambda e: e.tensor_copy(out=out, in_=in_), reads=reads, writes=writes)

    def MM(out, lhsT, rhs, start, stop, reads, writes):
        P.add("pe", lambda e: e.matmul(out, lhsT, rhs, start=start, stop=stop), reads=reads, writes=writes)

    def MMS(lst, reads, writes):
        def fn(e):
            ins = None
            for (o_, l_, r_, st_, sp_) in lst:
                ins = e.matmul(o_, l_, r_, start=st_, stop=sp_)
            return ins
        P.add("pe", fn, reads=reads, writes=writes)

    def DMA(q, out, in_, reads=(), writes=()):
        P.add(q, lambda e: e.dma_start(out=out, in_=in_), reads=reads, writes=writes)

    def wload(src_ap):
        s = state["wslot"]; state["wslot"] = (s + 1) % NSLOT
        dst = wring[s]
        src = src_ap.rearrange("(c p) n -> p c n", p=128)
        P.add("pool", lambda e, dst=dst, src=src: e.dma_start(out=dst, in_=src), writes=[("w", s)])
        return s

    def next_banks():
        b = state["psset"]; state["psset"] = 1 - b
        return b * 4

    def gemm_feat(src, srckey, nch, W, col0, ncols, evac):
        for g in range(ncols // 512):
            b0 = next_banks()
            c0 = col0 + g * 512
            for cq in range(nch // 4):
                s = wload(W[cq * 512:(cq + 1) * 512, c0:c0 + 512])

                def fn(e, s=s, cq=cq, b0=b0):
                    ins = None
                    for ci in range(4):
                        c = cq * 4 + ci
                        for jj in range(4):
                            ins = e.matmul(psum[:, b0 + jj, :], wring[s][:, ci, jj * 128:(jj + 1) * 128],
                                           src[:, c, :], start=(c == 0), stop=(c == nch - 1))
                    return ins
                P.add("pe", fn, reads=[("w", s), srckey], writes=[("ps", b0 + j) for j in range(4)])
            evac(g, b0)

    def gemm_tok(src, srckey, nch, W, col0, ncols, evac):
        for g in range(ncols // 512):
            b0 = next_banks()
            c0 = col0 + g * 512
            for cq in range(nch // 4):
                s = wload(W[cq * 512:(cq + 1) * 512, c0:c0 + 512])

                def fn(e, s=s, cq=cq, b0=b0):
                    ins = None
                    for ci in range(4):
                        c = cq * 4 + ci
                        for st in range(4):
                            ins = e.matmul(psum[:, b0 + st, :], src[:, c, st * 128:(st + 1) * 128],
                                           wring[s][:, ci, :], start=(c == 0), stop=(c == nch - 1))
                    return ins
                P.add("pe", fn, reads=[("w", s), srckey], writes=[("ps", b0 + j) for j in range(4)])
            evac(g, b0)

    def pskeys(b0, n=4):
        return [("ps", b0 + j) for j in range(n)]

    def evac_residual(alpha):
        def ev(g, b0):
            hv = h_sb[:, :, g * 512:(g + 1) * 512]
            P.add("dve", lambda e: e.scalar_tensor_tensor(out=hv, in0=psum[:, b0:b0 + 4, :], scalar=float(alpha),
                                                          in1=hv, op0=ALU.mult, op1=ALU.add),
                  reads=pskeys(b0) + [("h", g)], writes=[("h", g)])
            for s in range(4):
                square_acc(s, g)
        return ev

    evac_toggle = {"i": 0}

    def evac_store_feat(dstT, tile):
        def ev(g, b0):
            si = state["stg"]; state["stg"] = 1 - si
            sb = stg[si]
            eng = "act" if (evac_toggle["i"] % 2 == 0) else "dve"
            evac_toggle["i"] += 1
            if eng == "act":
                P.add("act", lambda e: e.activation(out=sb, in_=psum[:, b0:b0 + 4, :], func=AF.Copy),
                      reads=pskeys(b0), writes=[("stg", si)])
            else:
                P.add("dve", lambda e: e.tensor_copy(out=sb, in_=psum[:, b0:b0 + 4, :]),
                      reads=pskeys(b0), writes=[("stg", si)])
            dst = dstT[g * 512:(g + 1) * 512, tile * TT:(tile + 1) * TT].rearrange("(j p) t -> p j t", p=128)
            P.add("sp", lambda e: e.dma_start(out=dst, in_=sb), reads=[("stg", si)], writes=[("dram", dstT.name, g, tile)])
        return ev

    def evac_store_tok(dst, tile):
        def ev(g, b0):
            si = state["stg"]; state["stg"] = 1 - si
            sb = stg[si]
            eng = "act" if (evac_toggle["i"] % 2 == 0) else "dve"
            evac_toggle["i"] += 1
            if eng == "act":
                P.add("act", lambda e: e.activation(out=sb, in_=psum[:, b0:b0 + 4, :], func=AF.Copy),
                      reads=pskeys(b0), writes=[("stg", si)])
            else:
                P.add("dve", lambda e: e.tensor_copy(out=sb, in_=psum[:, b0:b0 + 4, :]),
                      reads=pskeys(b0), writes=[("stg", si)])
            d = dst[tile * TT:(tile + 1) * TT, g * 512:(g + 1) * 512].rearrange("(s p) c -> p s c", p=128)
            P.add("sp", lambda e: e.dma_start(out=d, in_=sb), reads=[("stg", si)], writes=[("dram", dst.name, g, tile)])
        return ev

    HKEYS = [("h", g) for g in range(8)]

    def norm_stage(gidx, final=False, presq=True):
        if not presq:
            for s in range(4):
                for g in range(8):
                    square_acc(s, g)
        P.add("dve", lambda e: e.tensor_reduce(out=stat[:, 32:36], in_=stat[:, 0:32].rearrange("p (s g) -> p s g", s=4),
                                               axis=AX.X, op=ALU.add),
              reads=[("stat", c) for c in range(32)], writes=[("stat", "ss")])
        P.add("dve", lambda e: e.tensor_scalar(out=stat[:, 36:40], in0=stat[:, 32:36], scalar1=1.0 / D, scalar2=EPS,
                                               op0=ALU.mult, op1=ALU.add),
              reads=[("stat", "ss")], writes=[("stat", "ms")])
        P.add("act", lambda e: e.activation(out=stat[:, 40:44], in_=stat[:, 36:40], func=AF.Sqrt),
              reads=[("stat", "ms")], writes=[("stat", "sd")])
        P.add("dve", lambda e: e.reciprocal(out=stat[:, 44:48], in_=stat[:, 40:44]),
              reads=[("stat", "sd")], writes=[("stat", "rstd")])
        if final:
            for s in range(4):
                P.add("dve", lambda e, s=s: e.scalar_tensor_tensor(
                    out=h_sb[:, s, :], in0=h_sb[:, s, :], scalar=stat[:, 44 + s:45 + s], in1=gfin,
                    op0=ALU.mult, op1=ALU.mult),
                    reads=HKEYS + [("stat", "rstd"), ("gfin",)], writes=HKEYS)
            return
        k = 0
        for s in range(4):
            for g in range(8):
                yi = k % 2
                bank = 0 if (k % 2 == 0) else 4
                k += 1
                P.add("act", lambda e, s=s, g=g, yi=yi: e.activation(
                    out=ys[:, yi, :], in_=h_sb[:, s, g * 512:(g + 1) * 512], func=AF.Copy,
                    scale=stat[:, 44 + s:45 + s]),
                    reads=[("h", g), ("stat", "rstd")], writes=[("ys", yi)])

                def tfn(e, yi=yi, bank=bank):
                    ins = None
                    for jj in range(4):
                        ins = e.transpose(out=psum[:, bank, jj * 128:(jj + 1) * 128],
                                          in_=ys[:, yi, jj * 128:(jj + 1) * 128], identity=ident)
                    return ins
                P.add("pe", tfn, reads=[("ys", yi), ("cf",)], writes=[("ps", bank)])
                for jj in range(4):
                    c = g * 4 + jj
                    P.add("dve", lambda e, c=c, s=s, jj=jj, bank=bank: e.tensor_scalar(
                        out=xnT[:, c, s * 128:(s + 1) * 128], in0=psum[:, bank, jj * 128:(jj + 1) * 128],
                        scalar1=gn[:, gidx * NCH + c:gidx * NCH + c + 1], scalar2=None, op0=ALU.mult),
                        reads=[("ps", bank), ("gn",)], writes=[("xnT",)])

    def ffn_half(l, i, presq=True):
        norm_stage(l * 2 + i, presq=presq)
        Win = w_ffn_in[l, i]
        Wout = w_ffn_out[l, i]
        for jg in range(FCH // 4):
            def ev_gate(g, b0):
                P.add("act", lambda e: e.activation(out=sil, in_=psum[:, b0:b0 + 4, :], func=AF.Silu),
                      reads=pskeys(b0), writes=[("sil",)])
            gemm_feat(xnT, ("xnT",), NCH, Win, jg * 512, 512, ev_gate)

            def ev_up(g, b0, jg=jg):
                P.add("dve", lambda e: e.tensor_tensor(out=actT[:, jg * 4:(jg + 1) * 4, :], in0=psum[:, b0:b0 + 4, :],
                                                       in1=sil, op=ALU.mult),
                      reads=pskeys(b0) + [("sil",)], writes=[("actT",)])
            gemm_feat(xnT, ("xnT",), NCH, Win, FF + jg * 512, 512, ev_up)
        gemm_tok(actT, ("actT",), FCH, Wout, 0, D, evac_residual(0.5))

    def load_h(src, tile):
        s_ap = src[tile * TT:(tile + 1) * TT, :].rearrange("(s p) d -> p s d", p=128)
        P.add("sp", lambda e: e.dma_start(out=h_sb, in_=s_ap), reads=[("dram", src.name, "h", tile)], writes=HKEYS)

    def store_h(dst, tile):
        d_ap = dst[tile * TT:(tile + 1) * TT, :].rearrange("(s p) d -> p s d", p=128)
        P.add("sp", lambda e: e.dma_start(out=d_ap, in_=h_sb), reads=HKEYS, writes=[("dram", dst.name, "h", tile)])

    def load_oT(tile):
        s_ap = oT_d[:, tile * TT:(tile + 1) * TT].rearrange("(c p) t -> p c t", p=128)
        P.add("sp", lambda e: e.dma_start(out=xnT, in_=s_ap), reads=[("dram", "oT")], writes=[("xnT",), ("gfin",)])

    def oproj(W, tile):
        if tile == 0:
            load_oT(tile)
        gemm_tok(xnT, ("xnT",), NCH, W, 0, D, evac_residual(1.0))

    def next_oT(tile):
        if tile + 1 < NT:
            load_oT(tile + 1)

    P.add("sp", lambda e: e.dma_start(out=arena[:, CF_OFF:CF_OFF + 3 * 128 * 4].bitcast(F32), in_=cf32_d), writes=[("cf",)])
    P.add("sp", lambda e: e.dma_start(out=gn, in_=gains_d), writes=[("gn",)])
    P.add("sp", lambda e: e.dma_start(out=gsub[:, 0:4], in_=gsub_d), writes=[("gsub",)])
    P.add("dve", lambda e: e.memset(onesb, 1.0), writes=[("onesb",)])

    a = 0
    AQ_OFF = a; a += 2 * 2 * T * 2
    AK_OFF = a; a += 2 * 2 * T * 2
    AV_OFF = a; a += 2 * NB * 256 * 2
    AO_OFF = a; a += 2 * 2 * T * 2
    MSK_OFF = a; a += 8 * 512 * 4
    TMP_OFF = a; a += 4 * 3 * 2048
    AB_OFF = a; a += 4 * 1024
    SSUM_OFF = a; a += 4 * 2048
    R_OFF = a; a += 3 * 2048
    OF_OFF = a; a += 2 * 2048
    SQ_OFF = a; a += 2 * 2048
    assert a <= CHAIN_END, a

    aq = [v3(AQ_OFF + b * 2 * T * 2, 2, T, BF16) for b in range(2)]
    ak = [v3(AK_OFF + b * 2 * T * 2, 2, T, BF16) for b in range(2)]
    av = [v3(AV_OFF + b * NB * 256 * 2, NB, 256, BF16) for b in range(2)]
    ao = [v3(AO_OFF + b * 2 * T * 2, 2, T, BF16) for b in range(2)]
    msk = v3(MSK_OFF, 8, 512, F32)
    tmp = [[v2(TMP_OFF + (p * 3 + i) * 2048, 512, F32) for i in range(3)] for p in range(4)]
    abf = [v2(AB_OFF + p * 1024, 512, BF16) for p in range(4)]
    ssums = [v2(SSUM_OFF + p * 2048, 512, F32) for p in range(4)]
    rbuf = [v2(R_OFF + i * 2048, 512, F32) for i in range(3)]
    ofp = v3(OF_OFF, 2, 512, F32)
    sqf = v3(SQ_OFF, 2, 512, F32)

    def load_masks(which):
        if which == "a":
            DMA("sp", arena[:, MSK_OFF:MSK_OFF + 8 * 512 * 4].bitcast(F32), cmask_d[:, 0:8 * 512], writes=[("msk",)])
        else:
            DMA("sp", arena[:, MSK_OFF:MSK_OFF + 5 * 512 * 4].bitcast(F32), cmask_d[:, 8 * 512:13 * 512], writes=[("msk",)])

    def attn_a(l):
        P.barrier()
        load_masks("a")
        H = 32
        NS = 4

        def load_head(h, b):
            DMA("sp", aq[b][:, 0, :], qT_d[h * 128:(h + 1) * 128, :], writes=[("aq", b)])
            DMA("sp", ak[b][:, 0, :], kT_d[h * 128:(h + 1) * 128, :], writes=[("ak", b)])
            DMA("sp", av[b][:, :, 0:128], v_d[:, h * 128:(h + 1) * 128].rearrange("(b p) d -> p b d", p=128),
                writes=[("av", b)])

        def stream(h, qt, slot):
            b = h % 2
            kbs = list(range(4 * qt + 3, -1, -1))
            zc = 2 * slot
            ob = 2 * slot + 1
            E, SPb, Wb = tmp[slot]
            A = abf[slot]
            ssum = ssums[slot]
            zps = psum[:, zc, :]
            qsl = aq[b][:, 0, qt * TT:(qt + 1) * TT]
            kE, kSP, kW, kA, kS = ("E", slot), ("SP", slot), ("W", slot), ("A", slot), ("ssum", slot)
            for idx, kb in enumerate(kbs):
                diag = kb >= 4 * qt
                oi = kb - 4 * qt
                last = idx == len(kbs) - 1
                MM(zps, ak[b][:, 0, kb * 128:(kb + 1) * 128], qsl, True, True,
                   reads=[("ak", b), ("aq", b)], writes=[("ps", zc)])
                yield
                ACT(E, zps, AF.Exp, reads=[("ps", zc)], writes=[kE], scale=SCALE)
                yield
                ACT(SPb, E, AF.Ln, reads=[kE], writes=[kSP], bias=1.0)
                yield
                STT(Wb, zps, SCALE, SPb, ALU.mult, ALU.subtract, reads=[("ps", zc), kSP], writes=[kW])
                if diag:
                    TTOP(SPb, SPb, msk[:, oi, :], ALU.mult, reads=[kSP, ("msk",)], writes=[kSP])
                    TTOP(Wb, Wb, msk[:, 4 + oi, :], ALU.add, reads=[kW, ("msk",)], writes=[kW])
                yield
                if idx == 0:
                    MM(zps, Umat, SPb, True, True, reads=[kSP, ("cf",)], writes=[("ps", zc)])
                else:
                    MMS([(zps, Umat, SPb, True, False), (zps, ones32, ssum, False, True)],
                        reads=[kSP, kS, ("cf",)], writes=[("ps", zc)])
                yield
                TTOP(Wb, Wb, zps, ALU.subtract, reads=[kW, ("ps", zc)], writes=[kW])
                if not last:
                    if idx == 0:
                        PCOPY(ssum, SPb, reads=[kSP], writes=[kS])
                    else:
                        PTT(ssum, ssum, SPb, ALU.add, reads=[kSP, kS], writes=[kS])
                yield
                ACT(A, Wb, AF.Exp, reads=[kW], writes=[kA])
                yield
                MM(psum[:, ob, :], av[b][:, kb, 0:128], A, idx == 0, last,
                   reads=[("av", b), kA], writes=[("ps", ob)])
                yield
            ACT(ao[b][:, 0, qt * TT:(qt + 1) * TT], psum[:, ob, :], AF.Copy, reads=[("ps", ob)], writes=[("ao", b, qt)])
            yield

        queue = [(h, qt) for h in range(H) for qt in range(NT)]
        queue.reverse()
        active = [None] * NS
        meta = [None] * NS
        done_cnt = {}
        load_head(0, 0)
        load_head(1, 1)
        loaded = {0, 1}
        while queue or any(a_ is not None for a_ in active):
            for slot in range(NS):
                if active[slot] is None and queue and queue[-1][0] in loaded:
                    h, qt = queue.pop()
                    active[slot] = stream(h, qt, slot)
                    meta[slot] = h
                if active[slot] is not None:
                    try:
                        next(active[slot])
                    except StopIteration:
                        h = meta[slot]
                        active[slot] = None
                        done_cnt[h] = done_cnt.get(h, 0) + 1
                        if done_cnt[h] == NT:
                            b = h % 2
                            DMA("sp", oT_d[h * 128:(h + 1) * 128, :], ao[b][:, 0, :],
                                reads=[("ao", b, q_) for q_ in range(NT)], writes=[("dram", "oT")])
                            if h + 2 < H:
                                load_head(h + 2, b)
                                loaded.add(h + 2)
        P.barrier()

    def attn_b(l):
        j = l - 2
        lam_init = lam_init_of(l)
        P.barrier()
        load_masks("b")
        lrep = v3(TMP_OFF, 4, 128, F32)
        LK = ("U", 0)
        DMA("sp", lrep, lamrep_d[:, j * 512:(j + 1) * 512].rearrange("p (w i) -> p w i", w=4), writes=[LK])
        TTOP(lrep[:, 0, :], lrep[:, 0, :], lrep[:, 1, :], ALU.mult, reads=[LK], writes=[LK])
        TTOP(lrep[:, 2, :], lrep[:, 2, :], lrep[:, 3, :], ALU.mult, reads=[LK], writes=[LK])
        P.add("dve", lambda e: e.tensor_reduce(out=lam_sb[:, 0:1], in_=lrep[:, 0, :], axis=AX.X, op=ALU.add),
              reads=[LK], writes=[("lam", 0)])
        P.add("dve", lambda e: e.tensor_reduce(out=lam_sb[:, 1:2], in_=lrep[:, 2, :], axis=AX.X, op=ALU.add),
              reads=[LK], writes=[("lam", 1)])
        ACT(lam_sb[:, 2:4], lam_sb[:, 0:2], AF.Exp, reads=[("lam", 0), ("lam", 1)], writes=[("lam", 2)])
        TTOP(lam_sb[:, 4:5], lam_sb[:, 2:3], lam_sb[:, 3:4], ALU.subtract, reads=[("lam", 2)], writes=[("lam", 4)])
        TS(lam_sb[:, 5:6], lam_sb[:, 4:5], float(lam_init), None, ALU.add, None, reads=[("lam", 4)], writes=[("lam", 5)])
        lam_ap = lam_sb[:, 5:6]
        H = 16

        def load_head(h, b):
            DMA("sp", aq[b], qT_d[h * 256:(h + 1) * 256, :].rearrange("(c p) t -> p c t", p=128), writes=[("aq", b)])
            DMA("sp", ak[b], kshT_d[h * 256:(h + 1) * 256, :].rearrange("(c p) t -> p c t", p=128), writes=[("ak", b)])
            DMA("sp", av[b], vsh_d[:, h * 256:(h + 1) * 256].rearrange("(b p) d -> p b d", p=128), writes=[("av", b)])
        load_head(0, 0)
        r1, r2l, rstd = rbuf
        ucnt = 0
        for h in range(H):
            b = h % 2
            slope = 2.0 ** (-8.0 * (h + 1) / 16.0)
            ch = -slope / SCALE
            if h + 1 < H:
                load_head(h + 1, 1 - b)
            for qt in range(NT):
                kbs = list(range(4 * qt + 3, -1, -1))
                units = [(idx, kb, c) for idx, kb in enumerate(kbs) for c in range(2)]
                pend = []

                def front(idx, kb, c, ui):
                    diag = kb >= 4 * qt
                    oi = kb - 4 * qt
                    delta = qt * TT - kb * 128
                    zb = c
                    Ub = tmp[ui % 4][0]
                    Eb = abf[ui % 4]
                    zps = psum[:, zb, :]
                    MM(zps, ak[b][:, c, kb * 128:(kb + 1) * 128], aq[b][:, c, qt * TT:(qt + 1) * TT], True, True,
                       reads=[("ak", b), ("aq", b)], writes=[("ps", zb)])
                    mt = msk[:, 1 + oi, :] if diag else msk[:, 0, :]
                    STT(Ub, mt, float(ch), zps, ALU.mult, ALU.add, reads=[("ps", zb), ("msk",)], writes=[("U", ui % 4)])
                    bconst = 0.0 if diag else float(-slope * delta)
                    ACT(Eb, Ub, AF.Exp, reads=[("U", ui % 4)], writes=[("Eb", ui % 4)], scale=SCALE, bias=bconst)

                def back(idx, kb, c, ui):
                    last = idx == len(kbs) - 1
                    Eb = abf[ui % 4]
                    MMS([(psum[:, 2 + 2 * c, :], av[b][:, kb, 0:128], Eb, idx == 0, last),
                         (psum[:, 3 + 2 * c, :], av[b][:, kb, 128:256], Eb, idx == 0, last),
                         (psum[:, 6 + c, :], onesb, Eb, idx == 0, last)],
                        reads=[("av", b), ("Eb", ui % 4), ("onesb",)],
                        writes=[("ps", 2 + 2 * c), ("ps", 3 + 2 * c), ("ps", 6 + c)])

                LAG = 1
                n = len(units)
                uis = []
                for i in range(n + LAG):
                    if i < n:
                        uis.append(ucnt)
                        front(*units[i], ucnt)
                        ucnt += 1
                    if i >= LAG:
                        back(*units[i - LAG], uis[i - LAG])
                RECIP(r1, psum[:, 6, :], reads=[("ps", 6)], writes=[("r", 0)])
                RECIP(r2l, psum[:, 7, :], reads=[("ps", 7)], writes=[("r", 1)])
                TS(r2l, r2l, lam_ap, None, ALU.mult, None, reads=[("r", 1), ("lam", 5)], writes=[("r", 1)])
                for half in range(2):
                    TTOP(ofp[:, half, :], psum[:, 2 + half, :], r1, ALU.mult, reads=[("ps", 2 + half), ("r", 0)],
                         writes=[("of", half)])
                    TTOP(sqf[:, half, :], psum[:, 4 + half, :], r2l, ALU.mult, reads=[("ps", 4 + half), ("r", 1)],
                         writes=[("sq", half)])
                    TTOP(ofp[:, half, :], ofp[:, half, :], sqf[:, half, :], ALU.subtract,
                         reads=[("of", half), ("sq", half)], writes=[("of", half)])
                ACT(sqf, ofp, AF.Square, reads=[("of", 0), ("of", 1), ("sq", 0), ("sq", 1)], writes=[("sq", 0), ("sq", 1)])
                MMS([(psum[:, 0, :], ones32, sqf[:, 0, :], True, False), (psum[:, 0, :], ones32, sqf[:, 1, :], False, True)],
                    reads=[("sq", 0), ("sq", 1), ("cf",)], writes=[("ps", 0)])
                TS(rstd, psum[:, 0, :], 1.0 / 256.0, EPS, ALU.mult, ALU.add, reads=[("ps", 0)], writes=[("r", 2)])
                ACT(rstd, rstd, AF.Sqrt, reads=[("r", 2)], writes=[("r", 2)])
                RECIP(rstd, rstd, reads=[("r", 2)], writes=[("r", 2)])
                TS(rstd, rstd, float(1.0 - lam_init), None, ALU.mult, None, reads=[("r", 2)], writes=[("r", 2)])
                for half in range(2):
                    STT(ao[b][:, half, qt * TT:(qt + 1) * TT], ofp[:, half, :], gsub[:, j * 2 + half:j * 2 + half + 1], rstd,
                        ALU.mult, ALU.mult, reads=[("of", half), ("r", 2), ("gsub",)], writes=[("ao", b)])
            DMA("sp", oT_d[h * 256:(h + 1) * 256, :].rearrange("(c p) t -> p c t", p=128), ao[b], reads=[("ao", b)],
                writes=[("dram", "oT")])
        P.barrier()

    def stop(name):
        return stop_after == name

    dbg_names = []

    def dump(name, tile):
        if not debug:
            return
        if name not in dbg_names:
            dbg_names.append(name)
        dd = nc.dram_tensor("dbg_" + name, [T, D], F32, kind="ExternalOutput").ap() if tile == 0 else dbg_aps[name]
        dbg_aps[name] = dd
        DMA("sp", dd[tile * TT:(tile + 1) * TT, :].rearrange("(s p) d -> p s d", p=128), h_sb, reads=HKEYS,
            writes=[("dram", "dbg", name, tile)])

    dbg_aps = {}

    def finish(tile):
        P.add("sp", lambda e: e.dma_start(out=out_d[tile * TT:(tile + 1) * TT, :].rearrange("(s p) d -> p s d", p=128),
                                          in_=h_sb),
              reads=HKEYS, writes=[("dram", "out", tile)])

    def qkv_a(l, tile):
        W = w_qkv_a[l]
        gemm_feat(xnT, ("xnT",), NCH, W, 0, D, evac_store_feat(qT_d, tile))
        gemm_feat(xnT, ("xnT",), NCH, W, D, D, evac_store_feat(kT_d, tile))
        gemm_tok(xnT, ("xnT",), NCH, W, 2 * D, D, evac_store_tok(v_d, tile))

    def dram_sync(names):
        pass

    def run():
        for tile in range(NT):
            load_h(x_d, tile)
            ffn_half(0, 0, presq=False)
            dump("ffn00", tile)
            if stop("ffn00"):
                finish(tile); continue
            norm_stage(8 + 0)
            qkv_a(0, tile)
            store_h(h_d, tile)
        if stop("ffn00"):
            return
        attn_a(0)
        for tile in range(NT):
            load_h(h_d, tile)
            oproj(w_o_a[0], tile)
            dump("attn0", tile)
            if stop("attn0"):
                finish(tile); continue
            ffn_half(0, 1)
            dump("ffn01", tile)
            ffn_half(1, 0)
            dump("ffn10", tile)
            norm_stage(8 + 1)
            qkv_a(1, tile)
            next_oT(tile)
            store_h(h_d, tile)
        if stop("attn0"):
            return
        attn_a(1)
        for tile in range(NT):
            load_h(h_d, tile)
            oproj(w_o_a[1], tile)
            dump("attn1", tile)
            ffn_half(1, 1)
            dump("ffn11", tile)
            if stop("ffn11"):
                finish(tile); continue
            norm_stage(12)
            gemm_feat(xnT, ("xnT",), NCH, w_kv_b, 0, D, evac_store_feat(kshT_d, tile))
            gemm_tok(xnT, ("xnT",), NCH, w_kv_b, D, D, evac_store_tok(vsh_d, tile))
            ffn_half(2, 0)
            dump("ffn20", tile)
            norm_stage(8 + 2)
            gemm_feat(xnT, ("xnT",), NCH, w_q_b[0], 0, D, evac_store_feat(qT_d, tile))
            next_oT(tile)
            store_h(h_d, tile)
        if stop("ffn11"):
            return
        attn_b(2)
        for tile in range(NT):
            load_h(h_d, tile)
            oproj(w_o_b[0], tile)
            dump("attn2", tile)
            if stop("attn2"):
                finish(tile); continue
            ffn_half(2, 1)
            dump("ffn21", tile)
            ffn_half(3, 0)
            dump("ffn30", tile)
            norm_stage(8 + 3)
            gemm_feat(xnT, ("xnT",), NCH, w_q_b[1], 0, D, evac_store_feat(qT_d, tile))
            next_oT(tile)
            store_h(h_d, tile)
        if stop("attn2"):
            return
        attn_b(3)
        for tile in range(NT):
            load_h(h_d, tile)
            oproj(w_o_b[1], tile)
            dump("attn3", tile)
            ffn_half(3, 1)
            dump("ffn31", tile)
            P.add("sp", lambda e: e.dma_start(out=gfin, in_=gfin_d), reads=[("xnT",)], writes=[("gfin",), ("xnT",)])
            norm_stage(13, final=True)
            next_oT(tile)
            finish(tile)

    run()
    P.barrier()

    sems = {}
    import contextlib
    with contextlib.ExitStack() as es:
        for e in ("pe", "act", "dve", "pc"):
            sems[e] = es.enter_context(nc.semaphore("s_" + e))
        for q in Prog.DMA:
            for r in range(RING):
                sems[(q, r)] = es.enter_context(nc.semaphore(f"s_{q}{r}"))
        block = es.enter_context(nc.Block())
        P.emit(nc, block, sems)
    return nc, P


def make_consts():
    p = np.arange(128)[:, None]
    f = np.arange(512)[None, :]
    ident = np.eye(128, dtype=np.float32)
    U = (np.arange(128)[:, None] > np.arange(128)[None, :]).astype(np.float32)
    ones = np.ones((128, 128), np.float32)
    cf32 = np.concatenate([ident, U, ones], axis=1)
    tiles = []
    for oi in range(4):
        tiles.append((f > p + oi * 128).astype(np.float32))
    for oi in range(4):
        tiles.append(np.where(f > p + oi * 128, 0.0, -30000.0).astype(np.float32))
    tiles.append((f - p).astype(np.float32) + np.zeros((128, 512), np.float32))
    for oi in range(4):
        s = p + oi * 128
        allowed = (s // 64) <= (f // 64)
        tiles.append(np.where(allowed, np.abs(f - s), 1.0e6).astype(np.float32))
    cmask = np.concatenate(tiles, axis=1)
    return np.ascontiguousarray(cf32), np.ascontiguousarray(cmask)


def layout_small(inputs):
    g_all = np.concatenate([
        np.asarray(inputs["ffn_norm"], np.float32).reshape(8, D),
        np.asarray(inputs["attn_norm"], np.float32).reshape(4, D),
        np.asarray(inputs["kv_norm"], np.float32).reshape(1, D),
        np.asarray(inputs["final_norm"], np.float32).reshape(1, D)], axis=0)
    gains = np.ascontiguousarray(g_all.reshape(14, NCH, 128).transpose(2, 0, 1).reshape(128, 14 * NCH))
    gfin = np.ascontiguousarray(np.broadcast_to(np.asarray(inputs["final_norm"], np.float32)[None, :], (128, D)))
    gsub = np.ascontiguousarray(np.asarray(inputs["subln_norm"], np.float32).reshape(2, 2, 128).transpose(2, 0, 1).reshape(128, 4))
    lam = np.stack([np.asarray(inputs[k], np.float32) for k in ("lambda_q1", "lambda_k1", "lambda_q2", "lambda_k2")], axis=1)
    lamrep = np.ascontiguousarray(np.broadcast_to(lam.reshape(1, 2 * 4 * 128), (128, 2 * 4 * 128)))
    return gains, gfin, gsub, lamrep


_CACHE = {}


def run_cores(inputs, NT, n_cores, stop_after=None, trace=False, debug=False):
    key = (NT, stop_after, debug)
    if key not in _CACHE:
        _CACHE[key] = build_program(NT, stop_after, debug)
    nc, _ = _CACHE[key]
    T = NT * TT
    gains, gfin, gsub, lamrep = layout_small(inputs)
    cf32, cmask = make_consts()
    x = np.asarray(inputs["x"], np.float32)
    shared = {
        "w_ffn_in": np.asarray(inputs["w_ffn_in"], np.float32),
        "w_ffn_out": np.asarray(inputs["w_ffn_out"], np.float32),
        "w_qkv_a": np.asarray(inputs["w_qkv_a"], np.float32),
        "w_o_a": np.asarray(inputs["w_o_a"], np.float32),
        "w_kv_b": np.asarray(inputs["w_kv_b"], np.float32),
        "w_q_b": np.asarray(inputs["w_q_b"], np.float32),
        "w_o_b": np.asarray(inputs["w_o_b"], np.float32),
        "gains": gains, "gfin": gfin, "gsub": gsub, "lamrep": lamrep, "cf32": cf32, "cmask": cmask,
    }
    in_maps = []
    for c in range(n_cores):
        m = dict(shared)
        m["x"] = np.ascontiguousarray(x[c, :T])
        in_maps.append(m)
    res = run_bass_kernel_spmd(nc, in_maps, core_ids=list(range(n_cores)), trace=trace)
    return res


def kernel(**inputs):
    res = run_cores(inputs, NT=4, n_cores=N_CORES)
    out = np.stack([np.asarray(r["out"], np.float32) for r in res.results], axis=0)
    return out
```

```python
import math
import numpy as np
import concourse.bass as bass
import concourse.mybir as mybir
from concourse.bass_utils import run_bass_kernel_spmd

F32 = mybir.dt.float32
BF16 = mybir.dt.bfloat16
U8 = mybir.dt.uint8
AF = mybir.ActivationFunctionType
ALU = mybir.AluOpType
AX = mybir.AxisListType

D = 4096
FF = 6144
NCH = D // 128
FCH = FF // 128
TT = 512
EPS = 1e-6
SCALE = 1.0 / math.sqrt(128.0)
NSLOT = 6
RING = 8
N_CORES = 8


class Prog:
    ENGS = ("pe", "act", "dve", "pc", "pool", "sp")
    DMA = ("pool", "sp")
    HWQ = {"pe": "tensor", "act": "scalar", "dve": "vector", "pc": "gpsimd", "pool": "gpsimd", "sp": "sync"}

    def __init__(self):
        self.seq = 0
        self.ops = {e: [] for e in self.ENGS}
        self.cnt = {e: 0 for e in self.ENGS}
        self.last_w = {}
        self.readers = {}
        self.waited = {e: {} for e in self.ENGS}

    def _need(self, eng, p, idx, waits):
        if p in self.DMA:
            i0 = idx - 1
            key = (p, i0 % RING)
            val = 16 * (i0 // RING + 1)
        else:
            if p == "pe" and eng == "pe":
                return
            key = p
            val = idx
        if self.waited[eng].get(key, 0) >= val:
            return
        self.waited[eng][key] = val
        waits.append((key, val))

    def add(self, eng, fn, reads=(), writes=()):
        n = self.cnt[eng] + 1
        self.cnt[eng] = n
        deps = set()
        lw = self.last_w
        rd = self.readers
        for k in reads:
            w = lw.get(k)
            if w is not None:
                deps.add(w)
        for k in writes:
            w = lw.get(k)
            if w is not None:
                deps.add(w)
            r = rd.get(k)
            if r:
                for e, i in r.items():
                    deps.add((e, i))
        waits = []
        if eng in self.DMA and n - 1 >= RING:
            self._need(eng, eng, n - RING, waits)
        for (p, idx) in sorted(deps):
            if p == eng and idx == n:
                continue
            self._need(eng, p, idx, waits)
        for k in reads:
            r = rd.get(k)
            if r is None:
                rd[k] = {eng: n}
            else:
                r[eng] = n
        for k in writes:
            lw[k] = (eng, n)
            rd[k] = {}
        self.seq += 1
        self.ops[eng].append((self.seq, waits, fn))
        return (eng, n)

    def barrier(self):
        snap = dict(self.cnt)
        for eng in self.ENGS:
            n = self.cnt[eng] + 1
            self.cnt[eng] = n
            waits = []
            for p in self.ENGS:
                if p == eng and p not in self.DMA:
                    continue
                c = snap[p]
                if c == 0:
                    continue
                if p in self.DMA:
                    for idx in range(max(1, c - RING + 1), c + 1):
                        self._need(eng, p, idx, waits)
                else:
                    self._need(eng, p, c, waits)
            self.seq += 1
            if eng in self.DMA:
                self.ops[eng].append((self.seq, waits, None))
                self.cnt[eng] = n - 1
            else:
                self.ops[eng].append((self.seq, waits, "nop"))

    def emit(self, nc, block, sems):
        hw = {"tensor": block.tensor, "scalar": block.scalar, "vector": block.vector,
              "gpsimd": block.gpsimd, "sync": block.sync}

        def make(queue):
            merged = []
            for eng in self.ENGS:
                if self.HWQ[eng] == queue:
                    merged.extend((seq, eng, waits, fn) for (seq, waits, fn) in self.ops[eng])
            merged.sort(key=lambda t: t[0])

            def body(e):
                k = {eng: 0 for eng in self.ENGS}
                for seq, eng, waits, fn in merged:
                    for key, val in waits:
                        e.wait_ge(sems[key], val)
                    if fn is None:
                        continue
                    if fn == "nop":
                        ins = e.nop()
                    else:
                        ins = fn(e)
                    if eng in self.DMA:
                        ins.then_inc(sems[(eng, k[eng] % RING)], 16)
                    else:
                        ins.then_inc(sems[eng], 1)
                    k[eng] += 1
            return body

        for queue in ("tensor", "scalar", "vector", "gpsimd", "sync"):
            hw[queue](make(queue))


def lam_init_of(l):
    return 0.8 - 0.6 * math.exp(-0.3 * l)


def build_program(NT, stop_after=None, debug=False):
    T = NT * TT
    NB = T // 128
    nc = bass.Bass("TRN2", target_bir_lowering=False)
    P = Prog()

    def din(name, shape, dt=F32):
        return nc.dram_tensor(name, list(shape), dt, kind="ExternalInput").ap()

    x_d = din("x", [T, D])
    w_ffn_in = din("w_ffn_in", [4, 2, D, 2 * FF])
    w_ffn_out = din("w_ffn_out", [4, 2, FF, D])
    w_qkv_a = din("w_qkv_a", [2, D, 3 * D])
    w_o_a = din("w_o_a", [2, D, D])
    w_kv_b = din("w_kv_b", [D, 2 * D])
    w_q_b = din("w_q_b", [2, D, D])
    w_o_b = din("w_o_b", [2, D, D])
    gains_d = din("gains", [128, 14 * NCH])
    gfin_d = din("gfin", [128, D])
    gsub_d = din("gsub", [128, 4])
    lamrep_d = din("lamrep", [128, 2 * 4 * 128])
    cf32_d = din("cf32", [128, 3 * 128])
    cmask_d = din("cmask", [128, 13 * 512])
    out_d = nc.dram_tensor("out", [T, D], F32, kind="ExternalOutput").ap()

    def dscr(name, shape, dt):
        return nc.dram_tensor(name, list(shape), dt, kind="Internal").ap()

    h_d = dscr("h_scr", [T, D], F32)
    qT_d = dscr("qT_scr", [D, T], BF16)
    kT_d = dscr("kT_scr", [D, T], BF16)
    v_d = dscr("v_scr", [T, D], BF16)
    kshT_d = dscr("kshT_scr", [D, T], BF16)
    vsh_d = dscr("vsh_scr", [T, D], BF16)
    oT_d = dscr("oT_scr", [D, T], BF16)

    ARENA = 206 * 1024
    arena = nc.alloc_sbuf_tensor("arena", [128, ARENA], U8)
    psum = nc.alloc_psum_tensor("psum", [128, 8, 512], F32)

    def view(off, nbytes, dt, **re):
        v = arena[:, off:off + nbytes].bitcast(dt)
        return v

    def v3(off, a, b, dt):
        sz = 4 if dt == F32 else 2
        v = arena[:, off:off + a * b * sz].bitcast(dt)
        return v.rearrange("p (a b) -> p a b", a=a)

    def v2(off, n, dt):
        sz = 4 if dt == F32 else 2
        return arena[:, off:off + n * sz].bitcast(dt)

    o = 0
    H_OFF = o; o += 4 * D * 4
    XN_OFF = o; o += NCH * TT * 2
    AC_OFF = o; o += FCH * TT * 2
    CHAIN_END = o
    WR_OFF = o; o += NSLOT * 4096
    SIL_OFF = o; o += 8192
    YS_OFF = o; o += 4096
    STG_OFF = o; o += 8192
    CF_OFF = o; o += 3 * 128 * 4
    ONB_OFF = o; o += 128 * 2
    GN_OFF = o; o += 14 * NCH * 4
    GSUB_OFF = o; o += 32
    SS_OFF = o; o += 64 * 4
    LAM_OFF = o; o += 64
    JUNK_OFF = o; o += 2 * 2048
    assert o <= ARENA, o

    h_sb = v3(H_OFF, 4, D, F32)
    xnT = v3(XN_OFF, NCH, TT, BF16)
    actT = v3(AC_OFF, FCH, TT, BF16)
    wring = [v3(WR_OFF + s * 4096, 4, 512, BF16) for s in range(NSLOT)]
    sil = v3(SIL_OFF, 4, 512, F32)
    ys = v3(YS_OFF, 2, 512, F32)
    stg = [v3(STG_OFF + s * 4096, 4, 512, BF16) for s in range(2)]
    cf = v3(CF_OFF, 3, 128, F32)
    ident, Umat, ones32 = cf[:, 0, :], cf[:, 1, :], cf[:, 2, :]
    onesb = v2(ONB_OFF, 128, BF16)
    gn = v2(GN_OFF, 14 * NCH, F32)
    gsub = v2(GSUB_OFF, 8, F32)
    stat = v2(SS_OFF, 64, F32)
    lam_sb = v2(LAM_OFF, 16, F32)
    junks = [v2(JUNK_OFF + i * 2048, 512, F32) for i in range(2)]
    jstate = {"i": 0}

    def square_acc(s, g):
        ji = jstate["i"]; jstate["i"] = 1 - ji
        col = s * 8 + g
        ACT(junks[ji], h_sb[:, s, g * 512:(g + 1) * 512], AF.Square, reads=[("h", g)],
            writes=[("junk", ji), ("stat", col)], accum_out=stat[:, col:col + 1])
    gfin = v2(XN_OFF, D, F32)

    state = {"wslot": 0, "stg": 0, "psset": 0}

    def ACT(out, in_, func, reads, writes, **kw):
        P.add("act", lambda e: e.activation(out=out, in_=in_, func=func, **kw), reads=reads, writes=writes)

    def STT(out, in0, scalar, in1, op0, op1, reads, writes):
        P.add("dve", lambda e: e.scalar_tensor_tensor(out=out, in0=in0, scalar=scalar, in1=in1, op0=op0, op1=op1),
              reads=reads, writes=writes)

    def TTOP(out, in0, in1, op, reads, writes):
        P.add("dve", lambda e: e.tensor_tensor(out=out, in0=in0, in1=in1, op=op), reads=reads, writes=writes)

    def TS(out, in0, s1, s2, op0, op1, reads, writes):
        if s2 is None:
            P.add("dve", lambda e: e.tensor_scalar(out=out, in0=in0, scalar1=s1, scalar2=None, op0=op0),
                  reads=reads, writes=writes)
        else:
            P.add("dve", lambda e: e.tensor_scalar(out=out, in0=in0, scalar1=s1, scalar2=s2, op0=op0, op1=op1),
                  reads=reads, writes=writes)

    def PTT(out, in0, in1, op, reads, writes):
        P.add("pc", lambda e: e.tensor_tensor(out=out, in0=in0, in1=in1, op=op), reads=reads, writes=writes)

    def PCOPY(out, in_, reads, writes):
        P.add("pc", lambda e: e.tensor_copy(out=out, in_=in_), reads=reads, writes=writes)

    def RECIP(out, in_, reads, writes):
        P.add("dve", lambda e: e.reciprocal(out=out, in_=in_), reads=reads, writes=writes)

    def COPY(out, in_, reads, writes):
        P.add("dve", lambda e: e.tensor_copy(out=out, in_=in_), reads=reads, writes=writes)

    def MM(out, lhsT, rhs, start, stop, reads, writes):
        P.add("pe", lambda e: e.matmul(out, lhsT, rhs, start=start, stop=stop), reads=reads, writes=writes)

    def MMS(lst, reads, writes):
        def fn(e):
            ins = None
            for (o_, l_, r_, st_, sp_) in lst:
                ins = e.matmul(o_, l_, r_, start=st_, stop=sp_)
            return ins
        P.add("pe", fn, reads=reads, writes=writes)

    def DMA(q, out, in_, reads=(), writes=()):
        P.add(q, lambda e: e.dma_start(out=out, in_=in_), reads=reads, writes=writes)

    def wload(src_ap):
        s = state["wslot"]; state["wslot"] = (s + 1) % NSLOT
        dst = wring[s]
        src = src_ap.rearrange("(c p) n -> p c n", p=128)
        P.add("pool", lambda e, dst=dst, src=src: e.dma_start(out=dst, in_=src), writes=[("w", s)])
        return s

    def next_banks():
        b = state["psset"]; state["psset"] = 1 - b
        return b * 4

    def gemm_feat(src, srckey, nch, W, col0, ncols, evac):
        for g in range(ncols // 512):
            b0 = next_banks()
            c0 = col0 + g * 512
            for cq in range(nch // 4):
                s = wload(W[cq * 512:(cq + 1) * 512, c0:c0 + 512])

                def fn(e, s=s, cq=cq, b0=b0):
                    ins = None
                    for ci in range(4):
                        c = cq * 4 + ci
                        for jj in range(4):
                            ins = e.matmul(psum[:, b0 + jj, :], wring[s][:, ci, jj * 128:(jj + 1) * 128],
                                           src[:, c, :], start=(c == 0), stop=(c == nch - 1))
                    return ins
                P.add("pe", fn, reads=[("w", s), srckey], writes=[("ps", b0 + j) for j in range(4)])
            evac(g, b0)

    def gemm_tok(src, srckey, nch, W, col0, ncols, evac):
        for g in range(ncols // 512):
            b0 = next_banks()
            c0 = col0 + g * 512
            for cq in range(nch // 4):
                s = wload(W[cq * 512:(cq + 1) * 512, c0:c0 + 512])

                def fn(e, s=s, cq=cq, b0=b0):
                    ins = None
                    for ci in range(4):
                        c = cq * 4 + ci
                        for st in range(4):
                            ins = e.matmul(psum[:, b0 + st, :], src[:, c, st * 128:(st + 1) * 128],
                                           wring[s][:, ci, :], start=(c == 0), stop=(c == nch - 1))
                    return ins
                P.add("pe", fn, reads=[("w", s), srckey], writes=[("ps", b0 + j) for j in range(4)])
            evac(g, b0)

    def pskeys(b0, n=4):
        return [("ps", b0 + j) for j in range(n)]

    def evac_residual(alpha):
        def ev(g, b0):
            hv = h_sb[:, :, g * 512:(g + 1) * 512]
            P.add("dve", lambda e: e.scalar_tensor_tensor(out=hv, in0=psum[:, b0:b0 + 4, :], scalar=float(alpha),
                                                          in1=hv, op0=ALU.mult, op1=ALU.add),
                  reads=pskeys(b0) + [("h", g)], writes=[("h", g)])
            for s in range(4):
                square_acc(s, g)
        return ev

    evac_toggle = {"i": 0}

    def evac_store_feat(dstT, tile):
        def ev(g, b0):
            si = state["stg"]; state["stg"] = 1 - si
            sb = stg[si]
            eng = "act" if (evac_toggle["i"] % 2 == 0) else "dve"
            evac_toggle["i"] += 1
            if eng == "act":
                P.add("act", lambda e: e.activation(out=sb, in_=psum[:, b0:b0 + 4, :], func=AF.Copy),
                      reads=pskeys(b0), writes=[("stg", si)])
            else:
                P.add("dve", lambda e: e.tensor_copy(out=sb, in_=psum[:, b0:b0 + 4, :]),
                      reads=pskeys(b0), writes=[("stg", si)])
            dst = dstT[g * 512:(g + 1) * 512, tile * TT:(tile + 1) * TT].rearrange("(j p) t -> p j t", p=128)
            P.add("sp", lambda e: e.dma_start(out=dst, in_=sb), reads=[("stg", si)], writes=[("dram", dstT.name, g, tile)])
        return ev

    def evac_store_tok(dst, tile):
        def ev(g, b0):
            si = state["stg"]; state["stg"] = 1 - si
            sb = stg[si]
            eng = "act" if (evac_toggle["i"] % 2 == 0) else "dve"
            evac_toggle["i"] += 1
            if eng == "act":
                P.add("act", lambda e: e.activation(out=sb, in_=psum[:, b0:b0 + 4, :], func=AF.Copy),
                      reads=pskeys(b0), writes=[("stg", si)])
            else:
                P.add("dve", lambda e: e.tensor_copy(out=sb, in_=psum[:, b0:b0 + 4, :]),
                      reads=pskeys(b0), writes=[("stg", si)])
            d = dst[tile * TT:(tile + 1) * TT, g * 512:(g + 1) * 512].rearrange("(s p) c -> p s c", p=128)
            P.add("sp", lambda e: e.dma_start(out=d, in_=sb), reads=[("stg", si)], writes=[("dram", dst.name, g, tile)])
        return ev

    HKEYS = [("h", g) for g in range(8)]

    def norm_stage(gidx, final=False, presq=True):
        if not presq:
            for s in range(4):
                for g in range(8):
                    square_acc(s, g)
        P.add("dve", lambda e: e.tensor_reduce(out=stat[:, 32:36], in_=stat[:, 0:32].rearrange("p (s g) -> p s g", s=4),
                                               axis=AX.X, op=ALU.add),
              reads=[("stat", c) for c in range(32)], writes=[("stat", "ss")])
        P.add("dve", lambda e: e.tensor_scalar(out=stat[:, 36:40], in0=stat[:, 32:36], scalar1=1.0 / D, scalar2=EPS,
                                               op0=ALU.mult, op1=ALU.add),
              reads=[("stat", "ss")], writes=[("stat", "ms")])
        P.add("act", lambda e: e.activation(out=stat[:, 40:44], in_=stat[:, 36:40], func=AF.Sqrt),
              reads=[("stat", "ms")], writes=[("stat", "sd")])
        P.add("dve", lambda e: e.reciprocal(out=stat[:, 44:48], in_=stat[:, 40:44]),
              reads=[("stat", "sd")], writes=[("stat", "rstd")])
        if final:
            for s in range(4):
                P.add("dve", lambda e, s=s: e.scalar_tensor_tensor(
                    out=h_sb[:, s, :], in0=h_sb[:, s, :], scalar=stat[:, 44 + s:45 + s], in1=gfin,
                    op0=ALU.mult, op1=ALU.mult),
                    reads=HKEYS + [("stat", "rstd"), ("gfin",)], writes=HKEYS)
            return
        k = 0
        for s in range(4):
            for g in range(8):
                yi = k % 2
                bank = 0 if (k % 2 == 0) else 4
                k += 1
                P.add("act", lambda e, s=s, g=g, yi=yi: e.activation(
                    out=ys[:, yi, :], in_=h_sb[:, s, g * 512:(g + 1) * 512], func=AF.Copy,
                    scale=stat[:, 44 + s:45 + s]),
                    reads=[("h", g), ("stat", "rstd")], writes=[("ys", yi)])

                def tfn(e, yi=yi, bank=bank):
                    ins = None
                    for jj in range(4):
                        ins = e.transpose(out=psum[:, bank, jj * 128:(jj + 1) * 128],
                                          in_=ys[:, yi, jj * 128:(jj + 1) * 128], identity=ident)
                    return ins
                P.add("pe", tfn, reads=[("ys", yi), ("cf",)], writes=[("ps", bank)])
                for jj in range(4):
                    c = g * 4 + jj
                    P.add("dve", lambda e, c=c, s=s, jj=jj, bank=bank: e.tensor_scalar(
                        out=xnT[:, c, s * 128:(s + 1) * 128], in0=psum[:, bank, jj * 128:(jj + 1) * 128],
                        scalar1=gn[:, gidx * NCH + c:gidx * NCH + c + 1], scalar2=None, op0=ALU.mult),
                        reads=[("ps", bank), ("gn",)], writes=[("xnT",)])

    def ffn_half(l, i, presq=True):
        norm_stage(l * 2 + i, presq=presq)
        Win = w_ffn_in[l, i]
        Wout = w_ffn_out[l, i]
        for jg in range(FCH // 4):
            def ev_gate(g, b0):
                P.add("act", lambda e: e.activation(out=sil, in_=psum[:, b0:b0 + 4, :], func=AF.Silu),
                      reads=pskeys(b0), writes=[("sil",)])
            gemm_feat(xnT, ("xnT",), NCH, Win, jg * 512, 512, ev_gate)

            def ev_up(g, b0, jg=jg):
                P.add("dve", lambda e: e.tensor_tensor(out=actT[:, jg * 4:(jg + 1) * 4, :], in0=psum[:, b0:b0 + 4, :],
                                                       in1=sil, op=ALU.mult),
                      reads=pskeys(b0) + [("sil",)], writes=[("actT",)])
            gemm_feat(xnT, ("xnT",), NCH, Win, FF + jg * 512, 512, ev_up)
        gemm_tok(actT, ("actT",), FCH, Wout, 0, D, evac_residual(0.5))

    def load_h(src, tile):
        s_ap = src[tile * TT:(tile + 1) * TT, :].rearrange("(s p) d -> p s d", p=128)
        P.add("sp", lambda e: e.dma_start(out=h_sb, in_=s_ap), reads=[("dram", src.name, "h", tile)], writes=HKEYS)

    def store_h(dst, tile):
        d_ap = dst[tile * TT:(tile + 1) * TT, :].rearrange("(s p) d -> p s d", p=128)
        P.add("sp", lambda e: e.dma_start(out=d_ap, in_=h_sb), reads=HKEYS, writes=[("dram", dst.name, "h", tile)])

    def load_oT(tile):
        s_ap = oT_d[:, tile * TT:(tile + 1) * TT].rearrange("(c p) t -> p c t", p=128)
        P.add("sp", lambda e: e.dma_start(out=xnT, in_=s_ap), reads=[("dram", "oT")], writes=[("xnT",), ("gfin",)])

    def oproj(W, tile):
        if tile == 0:
            load_oT(tile)
        gemm_tok(xnT, ("xnT",), NCH, W, 0, D, evac_residual(1.0))

    def next_oT(tile):
        if tile + 1 < NT:
            load_oT(tile + 1)

    P.add("sp", lambda e: e.dma_start(out=arena[:, CF_OFF:CF_OFF + 3 * 128 * 4].bitcast(F32), in_=cf32_d), writes=[("cf",)])
    P.add("sp", lambda e: e.dma_start(out=gn, in_=gains_d), writes=[("gn",)])
    P.add("sp", lambda e: e.dma_start(out=gsub[:, 0:4], in_=gsub_d), writes=[("gsub",)])
    P.add("dve", lambda e: e.memset(onesb, 1.0), writes=[("onesb",)])

    a = 0
    AQ_OFF = a; a += 2 * 2 * T * 2
    AK_OFF = a; a += 2 * 2 * T * 2
    AV_OFF = a; a += 2 * NB * 256 * 2
    AO_OFF = a; a += 2 * 2 * T * 2
    MSK_OFF = a; a += 8 * 512 * 4
    TMP_OFF = a; a += 4 * 3 * 2048
    AB_OFF = a; a += 4 * 1024
    SSUM_OFF = a; a += 4 * 2048
    R_OFF = a; a += 3 * 2048
    OF_OFF = a; a += 2 * 2048
    SQ_OFF = a; a += 2 * 2048
    assert a <= CHAIN_END, a

    aq = [v3(AQ_OFF + b * 2 * T * 2, 2, T, BF16) for b in range(2)]
    ak = [v3(AK_OFF + b * 2 * T * 2, 2, T, BF16) for b in range(2)]
    av = [v3(AV_OFF + b * NB * 256 * 2, NB, 256, BF16) for b in range(2)]
    ao = [v3(AO_OFF + b * 2 * T * 2, 2, T, BF16) for b in range(2)]
    msk = v3(MSK_OFF, 8, 512, F32)
    tmp = [[v2(TMP_OFF + (p * 3 + i) * 2048, 512, F32) for i in range(3)] for p in range(4)]
    abf = [v2(AB_OFF + p * 1024, 512, BF16) for p in range(4)]
    ssums = [v2(SSUM_OFF + p * 2048, 512, F32) for p in range(4)]
    rbuf = [v2(R_OFF + i * 2048, 512, F32) for i in range(3)]
    ofp = v3(OF_OFF, 2, 512, F32)
    sqf = v3(SQ_OFF, 2, 512, F32)

    def load_masks(which):
        if which == "a":
            DMA("sp", arena[:, MSK_OFF:MSK_OFF + 8 * 512 * 4].bitcast(F32), cmask_d[:, 0:8 * 512], writes=[("msk",)])
        else:
            DMA("sp", arena[:, MSK_OFF:MSK_OFF + 5 * 512 * 4].bitcast(F32), cmask_d[:, 8 * 512:13 * 512], writes=[("msk",)])

    def attn_a(l):
        P.barrier()
        load_masks("a")
        H = 32
        NS = 4

        def load_head(h, b):
            DMA("sp", aq[b][:, 0, :], qT_d[h * 128:(h + 1) * 128, :], writes=[("aq", b)])
            DMA("sp", ak[b][:, 0, :], kT_d[h * 128:(h + 1) * 128, :], writes=[("ak", b)])
            DMA("sp", av[b][:, :, 0:128], v_d[:, h * 128:(h + 1) * 128].rearrange("(b p) d -> p b d", p=128),
                writes=[("av", b)])

        def stream(h, qt, slot):
            b = h % 2
            kbs = list(range(4 * qt + 3, -1, -1))
            zc = 2 * slot
            ob = 2 * slot + 1
            E, SPb, Wb = tmp[slot]
            A = abf[slot]
            ssum = ssums[slot]
            zps = psum[:, zc, :]
            qsl = aq[b][:, 0, qt * TT:(qt + 1) * TT]
            kE, kSP, kW, kA, kS = ("E", slot), ("SP", slot), ("W", slot), ("A", slot), ("ssum", slot)
            for idx, kb in enumerate(kbs):
                diag = kb >= 4 * qt
                oi = kb - 4 * qt
                last = idx == len(kbs) - 1
                MM(zps, ak[b][:, 0, kb * 128:(kb + 1) * 128], qsl, True, True,
                   reads=[("ak", b), ("aq", b)], writes=[("ps", zc)])
                yield
                ACT(E, zps, AF.Exp, reads=[("ps", zc)], writes=[kE], scale=SCALE)
                yield
                ACT(SPb, E, AF.Ln, reads=[kE], writes=[kSP], bias=1.0)
                yield
                STT(Wb, zps, SCALE, SPb, ALU.mult, ALU.subtract, reads=[("ps", zc), kSP], writes=[kW])
                if diag:
                    TTOP(SPb, SPb, msk[:, oi, :], ALU.mult, reads=[kSP, ("msk",)], writes=[kSP])
                    TTOP(Wb, Wb, msk[:, 4 + oi, :], ALU.add, reads=[kW, ("msk",)], writes=[kW])
                yield
                if idx == 0:
                    MM(zps, Umat, SPb, True, True, reads=[kSP, ("cf",)], writes=[("ps", zc)])
                else:
                    MMS([(zps, Umat, SPb, True, False), (zps, ones32, ssum, False, True)],
                        reads=[kSP, kS, ("cf",)], writes=[("ps", zc)])
                yield
                TTOP(Wb, Wb, zps, ALU.subtract, reads=[kW, ("ps", zc)], writes=[kW])
                if not last:
                    if idx == 0:
                        COPY(ssum, SPb, reads=[kSP], writes=[kS])
                    else:
                        TTOP(ssum, ssum, SPb, ALU.add, reads=[kSP, kS], writes=[kS])
                yield
                ACT(A, Wb, AF.Exp, reads=[kW], writes=[kA])
                yield
                MM(psum[:, ob, :], av[b][:, kb, 0:128], A, idx == 0, last,
                   reads=[("av", b), kA], writes=[("ps", ob)])
                yield
            ACT(ao[b][:, 0, qt * TT:(qt + 1) * TT], psum[:, ob, :], AF.Copy, reads=[("ps", ob)], writes=[("ao", b, qt)])
            yield

        queue = [(h, qt) for h in range(H) for qt in range(NT)]
        queue.reverse()
        active = [None] * NS
        meta = [None] * NS
        done_cnt = {}
        load_head(0, 0)
        load_head(1, 1)
        loaded = {0, 1}
        while queue or any(a_ is not None for a_ in active):
            for slot in range(NS):
                if active[slot] is None and queue and queue[-1][0] in loaded:
                    h, qt = queue.pop()
                    active[slot] = stream(h, qt, slot)
                    meta[slot] = h
                if active[slot] is not None:
                    try:
                        next(active[slot])
                    except StopIteration:
                        h = meta[slot]
                        active[slot] = None
                        done_cnt[h] = done_cnt.get(h, 0) + 1
                        if done_cnt[h] == NT:
                            b = h % 2
                            DMA("sp", oT_d[h * 128:(h + 1) * 128, :], ao[b][:, 0, :],
                                reads=[("ao", b, q_) for q_ in range(NT)], writes=[("dram", "oT")])
                            if h + 2 < H:
                                load_head(h + 2, b)
                                loaded.add(h + 2)
        P.barrier()

    def attn_b(l):
        j = l - 2
        lam_init = lam_init_of(l)
        P.barrier()
        load_masks("b")
        lrep = v3(TMP_OFF, 4, 128, F32)
        LK = ("U", 0)
        DMA("sp", lrep, lamrep_d[:, j * 512:(j + 1) * 512].rearrange("p (w i) -> p w i", w=4), writes=[LK])
        TTOP(lrep[:, 0, :], lrep[:, 0, :], lrep[:, 1, :], ALU.mult, reads=[LK], writes=[LK])
        TTOP(lrep[:, 2, :], lrep[:, 2, :], lrep[:, 3, :], ALU.mult, reads=[LK], writes=[LK])
        P.add("dve", lambda e: e.tensor_reduce(out=lam_sb[:, 0:1], in_=lrep[:, 0, :], axis=AX.X, op=ALU.add),
              reads=[LK], writes=[("lam", 0)])
        P.add("dve", lambda e: e.tensor_reduce(out=lam_sb[:, 1:2], in_=lrep[:, 2, :], axis=AX.X, op=ALU.add),
              reads=[LK], writes=[("lam", 1)])
        ACT(lam_sb[:, 2:4], lam_sb[:, 0:2], AF.Exp, reads=[("lam", 0), ("lam", 1)], writes=[("lam", 2)])
        TTOP(lam_sb[:, 4:5], lam_sb[:, 2:3], lam_sb[:, 3:4], ALU.subtract, reads=[("lam", 2)], writes=[("lam", 4)])
        TS(lam_sb[:, 5:6], lam_sb[:, 4:5], float(lam_init), None, ALU.add, None, reads=[("lam", 4)], writes=[("lam", 5)])
        lam_ap = lam_sb[:, 5:6]
        H = 16

        def load_head(h, b):
            DMA("sp", aq[b], qT_d[h * 256:(h + 1) * 256, :].rearrange("(c p) t -> p c t", p=128), writes=[("aq", b)])
            DMA("sp", ak[b], kshT_d[h * 256:(h + 1) * 256, :].rearrange("(c p) t -> p c t", p=128), writes=[("ak", b)])
            DMA("sp", av[b], vsh_d[:, h * 256:(h + 1) * 256].rearrange("(b p) d -> p b d", p=128), writes=[("av", b)])
        load_head(0, 0)
        r1, r2l, rstd = rbuf
        ucnt = 0
        for h in range(H):
            b = h % 2
            slope = 2.0 ** (-8.0 * (h + 1) / 16.0)
            ch = -slope / SCALE
            if h + 1 < H:
                load_head(h + 1, 1 - b)
            for qt in range(NT):
                kbs = list(range(4 * qt + 3, -1, -1))
                units = [(idx, kb, c) for idx, kb in enumerate(kbs) for c in range(2)]
                pend = []

                def front(idx, kb, c, ui):
                    diag = kb >= 4 * qt
                    oi = kb - 4 * qt
                    delta = qt * TT - kb * 128
                    zb = c
                    Ub = tmp[ui % 4][0]
                    Eb = abf[ui % 4]
                    zps = psum[:, zb, :]
                    MM(zps, ak[b][:, c, kb * 128:(kb + 1) * 128], aq[b][:, c, qt * TT:(qt + 1) * TT], True, True,
                       reads=[("ak", b), ("aq", b)], writes=[("ps", zb)])
                    mt = msk[:, 1 + oi, :] if diag else msk[:, 0, :]
                    STT(Ub, mt, float(ch), zps, ALU.mult, ALU.add, reads=[("ps", zb), ("msk",)], writes=[("U", ui % 4)])
                    bconst = 0.0 if diag else float(-slope * delta)
                    ACT(Eb, Ub, AF.Exp, reads=[("U", ui % 4)], writes=[("Eb", ui % 4)], scale=SCALE, bias=bconst)

                def back(idx, kb, c, ui):
                    last = idx == len(kbs) - 1
                    Eb = abf[ui % 4]
                    MMS([(psum[:, 2 + 2 * c, :], av[b][:, kb, 0:128], Eb, idx == 0, last),
                         (psum[:, 3 + 2 * c, :], av[b][:, kb, 128:256], Eb, idx == 0, last),
                         (psum[:, 6 + c, :], onesb, Eb, idx == 0, last)],
                        reads=[("av", b), ("Eb", ui % 4), ("onesb",)],
                        writes=[("ps", 2 + 2 * c), ("ps", 3 + 2 * c), ("ps", 6 + c)])

                LAG = 1
                n = len(units)
                uis = []
                for i in range(n + LAG):
                    if i < n:
                        uis.append(ucnt)
                        front(*units[i], ucnt)
                        ucnt += 1
                    if i >= LAG:
                        back(*units[i - LAG], uis[i - LAG])
                RECIP(r1, psum[:, 6, :], reads=[("ps", 6)], writes=[("r", 0)])
                RECIP(r2l, psum[:, 7, :], reads=[("ps", 7)], writes=[("r", 1)])
                TS(r2l, r2l, lam_ap, None, ALU.mult, None, reads=[("r", 1), ("lam", 5)], writes=[("r", 1)])
                for half in range(2):
                    TTOP(ofp[:, half, :], psum[:, 2 + half, :], r1, ALU.mult, reads=[("ps", 2 + half), ("r", 0)],
                         writes=[("of", half)])
                    TTOP(sqf[:, half, :], psum[:, 4 + half, :], r2l, ALU.mult, reads=[("ps", 4 + half), ("r", 1)],
                         writes=[("sq", half)])
                    TTOP(ofp[:, half, :], ofp[:, half, :], sqf[:, half, :], ALU.subtract,
                         reads=[("of", half), ("sq", half)], writes=[("of", half)])
                ACT(sqf, ofp, AF.Square, reads=[("of", 0), ("of", 1), ("sq", 0), ("sq", 1)], writes=[("sq", 0), ("sq", 1)])
                MMS([(psum[:, 0, :], ones32, sqf[:, 0, :], True, False), (psum[:, 0, :], ones32, sqf[:, 1, :], False, True)],
                    reads=[("sq", 0), ("sq", 1), ("cf",)], writes=[("ps", 0)])
                TS(rstd, psum[:, 0, :], 1.0 / 256.0, EPS, ALU.mult, ALU.add, reads=[("ps", 0)], writes=[("r", 2)])
                ACT(rstd, rstd, AF.Sqrt, reads=[("r", 2)], writes=[("r", 2)])
                RECIP(rstd, rstd, reads=[("r", 2)], writes=[("r", 2)])
                TS(rstd, rstd, float(1.0 - lam_init), None, ALU.mult, None, reads=[("r", 2)], writes=[("r", 2)])
                for half in range(2):
                    STT(ao[b][:, half, qt * TT:(qt + 1) * TT], ofp[:, half, :], gsub[:, j * 2 + half:j * 2 + half + 1], rstd,
                        ALU.mult, ALU.mult, reads=[("of", half), ("r", 2), ("gsub",)], writes=[("ao", b)])
            DMA("sp", oT_d[h * 256:(h + 1) * 256, :].rearrange("(c p) t -> p c t", p=128), ao[b], reads=[("ao", b)],
                writes=[("dram", "oT")])
        P.barrier()

    def stop(name):
        return stop_after == name

    dbg_names = []

    def dump(name, tile):
        if not debug:
            return
        if name not in dbg_names:
            dbg_names.append(name)
        dd = nc.dram_tensor("dbg_" + name, [T, D], F32, kind="ExternalOutput").ap() if tile == 0 else dbg_aps[name]
        dbg_aps[name] = dd
        DMA("sp", dd[tile * TT:(tile + 1) * TT, :].rearrange("(s p) d -> p s d", p=128), h_sb, reads=HKEYS,
            writes=[("dram", "dbg", name, tile)])

    dbg_aps = {}

    def finish(tile):
        P.add("sp", lambda e: e.dma_start(out=out_d[tile * TT:(tile + 1) * TT, :].rearrange("(s p) d -> p s d", p=128),
                                          in_=h_sb),
              reads=HKEYS, writes=[("dram", "out", tile)])

    def qkv_a(l, tile):
        W = w_qkv_a[l]
        gemm_feat(xnT, ("xnT",), NCH, W, 0, D, evac_store_feat(qT_d, tile))
        gemm_feat(xnT, ("xnT",), NCH, W, D, D, evac_store_feat(kT_d, tile))
        gemm_tok(xnT, ("xnT",), NCH, W, 2 * D, D, evac_store_tok(v_d, tile))

    def dram_sync(names):
        pass

    def run():
        for tile in range(NT):
            load_h(x_d, tile)
            ffn_half(0, 0, presq=False)
            dump("ffn00", tile)
            if stop("ffn00"):
                finish(tile); continue
            norm_stage(8 + 0)
            qkv_a(0, tile)
            store_h(h_d, tile)
        if stop("ffn00"):
            return
        attn_a(0)
        for tile in range(NT):
            load_h(h_d, tile)
            oproj(w_o_a[0], tile)
            dump("attn0", tile)
            if stop("attn0"):
                finish(tile); continue
            ffn_half(0, 1)
            dump("ffn01", tile)
            ffn_half(1, 0)
            dump("ffn10", tile)
            norm_stage(8 + 1)
            qkv_a(1, tile)
            next_oT(tile)
            store_h(h_d, tile)
        if stop("attn0"):
            return
        attn_a(1)
        for tile in range(NT):
            load_h(h_d, tile)
            oproj(w_o_a[1], tile)
            dump("attn1", tile)
            ffn_half(1, 1)
            dump("ffn11", tile)
            if stop("ffn11"):
                finish(tile); continue
            norm_stage(12)
            gemm_feat(xnT, ("xnT",), NCH, w_kv_b, 0, D, evac_store_feat(kshT_d, tile))
            gemm_tok(xnT, ("xnT",), NCH, w_kv_b, D, D, evac_store_tok(vsh_d, tile))
            ffn_half(2, 0)
            dump("ffn20", tile)
            norm_stage(8 + 2)
            gemm_feat(xnT, ("xnT",), NCH, w_q_b[0], 0, D, evac_store_feat(qT_d, tile))
            next_oT(tile)
            store_h(h_d, tile)
        if stop("ffn11"):
            return
        attn_b(2)
        for tile in range(NT):
            load_h(h_d, tile)
            oproj(w_o_b[0], tile)
            dump("attn2", tile)
            if stop("attn2"):
                finish(tile); continue
            ffn_half(2, 1)
            dump("ffn21", tile)
            ffn_half(3, 0)
            dump("ffn30", tile)
            norm_stage(8 + 3)
            gemm_feat(xnT, ("xnT",), NCH, w_q_b[1], 0, D, evac_store_feat(qT_d, tile))
            next_oT(tile)
            store_h(h_d, tile)
        if stop("attn2"):
            return
        attn_b(3)
        for tile in range(NT):
            load_h(h_d, tile)
            oproj(w_o_b[1], tile)
            dump("attn3", tile)
            ffn_half(3, 1)
            dump("ffn31", tile)
            P.add("sp", lambda e: e.dma_start(out=gfin, in_=gfin_d), reads=[("xnT",)], writes=[("gfin",), ("xnT",)])
            norm_stage(13, final=True)
            next_oT(tile)
            finish(tile)

    run()
    P.barrier()

    sems = {}
    import contextlib
    with contextlib.ExitStack() as es:
        for e in ("pe", "act", "dve", "pc"):
            sems[e] = es.enter_context(nc.semaphore("s_" + e))
        for q in Prog.DMA:
            for r in range(RING):
                sems[(q, r)] = es.enter_context(nc.semaphore(f"s_{q}{r}"))
        block = es.enter_context(nc.Block())
        P.emit(nc, block, sems)
    return nc, P


def make_consts():
    p = np.arange(128)[:, None]
    f = np.arange(512)[None, :]
    ident = np.eye(128, dtype=np.float32)
    U = (np.arange(128)[:, None] > np.arange(128)[None, :]).astype(np.float32)
    ones = np.ones((128, 128), np.float32)
    cf32 = np.concatenate([ident, U, ones], axis=1)
    tiles = []
    for oi in range(4):
        tiles.append((f > p + oi * 128).astype(np.float32))
    for oi in range(4):
        tiles.append(np.where(f > p + oi * 128, 0.0, -30000.0).astype(np.float32))
    tiles.append((f - p).astype(np.float32) + np.zeros((128, 512), np.float32))
    for oi in range(4):
        s = p + oi * 128
        allowed = (s // 64) <= (f // 64)
        tiles.append(np.where(allowed, np.abs(f - s), 1.0e6).astype(np.float32))
    cmask = np.concatenate(tiles, axis=1)
    return np.ascontiguousarray(cf32), np.ascontiguousarray(cmask)


def layout_small(inputs):
    g_all = np.concatenate([
        np.asarray(inputs["ffn_norm"], np.float32).reshape(8, D),
        np.asarray(inputs["attn_norm"], np.float32).reshape(4, D),
        np.asarray(inputs["kv_norm"], np.float32).reshape(1, D),
        np.asarray(inputs["final_norm"], np.float32).reshape(1, D)], axis=0)
    gains = np.ascontiguousarray(g_all.reshape(14, NCH, 128).transpose(2, 0, 1).reshape(128, 14 * NCH))
    gfin = np.ascontiguousarray(np.broadcast_to(np.asarray(inputs["final_norm"], np.float32)[None, :], (128, D)))
    gsub = np.ascontiguousarray(np.asarray(inputs["subln_norm"], np.float32).reshape(2, 2, 128).transpose(2, 0, 1).reshape(128, 4))
    lam = np.stack([np.asarray(inputs[k], np.float32) for k in ("lambda_q1", "lambda_k1", "lambda_q2", "lambda_k2")], axis=1)
    lamrep = np.ascontiguousarray(np.broadcast_to(lam.reshape(1, 2 * 4 * 128), (128, 2 * 4 * 128)))
    return gains, gfin, gsub, lamrep


_CACHE = {}


def run_cores(inputs, NT, n_cores, stop_after=None, trace=False, debug=False):
    key = (NT, stop_after, debug)
    if key not in _CACHE:
        _CACHE[key] = build_program(NT, stop_after, debug)
    nc, _ = _CACHE[key]
    T = NT * TT
    gains, gfin, gsub, lamrep = layout_small(inputs)
    cf32, cmask = make_consts()
    x = np.asarray(inputs["x"], np.float32)
    shared = {
        "w_ffn_in": np.asarray(inputs["w_ffn_in"], np.float32),
        "w_ffn_out": np.asarray(inputs["w_ffn_out"], np.float32),
        "w_qkv_a": np.asarray(inputs["w_qkv_a"], np.float32),
        "w_o_a": np.asarray(inputs["w_o_a"], np.float32),
        "w_kv_b": np.asarray(inputs["w_kv_b"], np.float32),
        "w_q_b": np.asarray(inputs["w_q_b"], np.float32),
        "w_o_b": np.asarray(inputs["w_o_b"], np.float32),
        "gains": gains, "gfin": gfin, "gsub": gsub, "lamrep": lamrep, "cf32": cf32, "cmask": cmask,
    }
    in_maps = []
    for c in range(n_cores):
        m = dict(shared)
        m["x"] = np.ascontiguousarray(x[c, :T])
        in_maps.append(m)
    res = run_bass_kernel_spmd(nc, in_maps, core_ids=list(range(n_cores)), trace=trace)
    return res


def kernel(**inputs):
    res = run_cores(inputs, NT=4, n_cores=N_CORES)
    out = np.stack([np.asarray(r["out"], np.float32) for r in res.results], axis=0)
    return out
```

```python
import math
import numpy as np
import concourse.bass as bass
import concourse.mybir as mybir
from concourse.bass_utils import run_bass_kernel_spmd

F32 = mybir.dt.float32
BF16 = mybir.dt.bfloat16
U8 = mybir.dt.uint8
AF = mybir.ActivationFunctionType
ALU = mybir.AluOpType
AX = mybir.AxisListType

D = 4096
FF = 6144
NCH = D // 128
FCH = FF // 128
TT = 512
EPS = 1e-6
SCALE = 1.0 / math.sqrt(128.0)
NSLOT = 6
RING = 8
N_CORES = 8


class Prog:
    ENGS = ("pe", "act", "dve", "pc", "pool", "sp")
    DMA = ("pool", "sp")
    HWQ = {"pe": "tensor", "act": "scalar", "dve": "vector", "pc": "gpsimd", "pool": "gpsimd", "sp": "sync"}

    def __init__(self):
        self.seq = 0
        self.ops = {e: [] for e in self.ENGS}
        self.cnt = {e: 0 for e in self.ENGS}
        self.last_w = {}
        self.readers = {}
        self.waited = {e: {} for e in self.ENGS}

    def _need(self, eng, p, idx, waits):
        if p in self.DMA:
            i0 = idx - 1
            key = (p, i0 % RING)
            val = 16 * (i0 // RING + 1)
        else:
            if p == "pe" and eng == "pe":
                return
            key = p
            val = idx
        if self.waited[eng].get(key, 0) >= val:
            return
        self.waited[eng][key] = val
        waits.append((key, val))

    def add(self, eng, fn, reads=(), writes=()):
        n = self.cnt[eng] + 1
        self.cnt[eng] = n
        deps = set()
        lw = self.last_w
        rd = self.readers
        for k in reads:
            w = lw.get(k)
            if w is not None:
                deps.add(w)
        for k in writes:
            w = lw.get(k)
            if w is not None:
                deps.add(w)
            r = rd.get(k)
            if r:
                for e, i in r.items():
                    deps.add((e, i))
        waits = []
        if eng in self.DMA and n - 1 >= RING:
            self._need(eng, eng, n - RING, waits)
        for (p, idx) in sorted(deps):
            if p == eng and idx == n:
                continue
            self._need(eng, p, idx, waits)
        for k in reads:
            r = rd.get(k)
            if r is None:
                rd[k] = {eng: n}
            else:
                r[eng] = n
        for k in writes:
            lw[k] = (eng, n)
            rd[k] = {}
        self.seq += 1
        self.ops[eng].append((self.seq, waits, fn))
        return (eng, n)

    def barrier(self):
        snap = dict(self.cnt)
        for eng in self.ENGS:
            n = self.cnt[eng] + 1
            self.cnt[eng] = n
            waits = []
            for p in self.ENGS:
                if p == eng and p not in self.DMA:
                    continue
                c = snap[p]
                if c == 0:
                    continue
                if p in self.DMA:
                    for idx in range(max(1, c - RING + 1), c + 1):
                        self._need(eng, p, idx, waits)
                else:
                    self._need(eng, p, c, waits)
            self.seq += 1
            if eng in self.DMA:
                self.ops[eng].append((self.seq, waits, None))
                self.cnt[eng] = n - 1
            else:
                self.ops[eng].append((self.seq, waits, "nop"))

    def emit(self, nc, block, sems):
        hw = {"tensor": block.tensor, "scalar": block.scalar, "vector": block.vector,
              "gpsimd": block.gpsimd, "sync": block.sync}

        def make(queue):
            merged = []
            for eng in self.ENGS:
                if self.HWQ[eng] == queue:
                    merged.extend((seq, eng, waits, fn) for (seq, waits, fn) in self.ops[eng])
            merged.sort(key=lambda t: t[0])

            def body(e):
                k = {eng: 0 for eng in self.ENGS}
                for seq, eng, waits, fn in merged:
                    for key, val in waits:
                        e.wait_ge(sems[key], val)
                    if fn is None:
                        continue
                    if fn == "nop":
                        ins = e.nop()
                    else:
                        ins = fn(e)
                    if eng in self.DMA:
                        ins.then_inc(sems[(eng, k[eng] % RING)], 16)
                    else:
                        ins.then_inc(sems[eng], 1)
                    k[eng] += 1
            return body

        for queue in ("tensor", "scalar", "vector", "gpsimd", "sync"):
            hw[queue](make(queue))


def lam_init_of(l):
    return 0.8 - 0.6 * math.exp(-0.3 * l)


def build_program(NT, stop_after=None, debug=False):
    T = NT * TT
    NB = T // 128
    nc = bass.Bass("TRN2", target_bir_lowering=False)
    P = Prog()

    def din(name, shape, dt=F32):
        return nc.dram_tensor(name, list(shape), dt, kind="ExternalInput").ap()

    x_d = din("x", [T, D])
    w_ffn_in = din("w_ffn_in", [4, 2, D, 2 * FF])
    w_ffn_out = din("w_ffn_out", [4, 2, FF, D])
    w_qkv_a = din("w_qkv_a", [2, D, 3 * D])
    w_o_a = din("w_o_a", [2, D, D])
    w_kv_b = din("w_kv_b", [D, 2 * D])
    w_q_b = din("w_q_b", [2, D, D])
    w_o_b = din("w_o_b", [2, D, D])
    gains_d = din("gains", [128, 14 * NCH])
    gfin_d = din("gfin", [128, D])
    gsub_d = din("gsub", [128, 4])
    lamrep_d = din("lamrep", [128, 2 * 4 * 128])
    cf32_d = din("cf32", [128, 3 * 128])
    cmask_d = din("cmask", [128, 13 * 512])
    out_d = nc.dram_tensor("out", [T, D], F32, kind="ExternalOutput").ap()

    def dscr(name, shape, dt):
        return nc.dram_tensor(name, list(shape), dt, kind="Internal").ap()

    h_d = dscr("h_scr", [T, D], F32)
    qT_d = dscr("qT_scr", [D, T], BF16)
    kT_d = dscr("kT_scr", [D, T], BF16)
    v_d = dscr("v_scr", [T, D], BF16)
    kshT_d = dscr("kshT_scr", [D, T], BF16)
    vsh_d = dscr("vsh_scr", [T, D], BF16)
    oT_d = dscr("oT_scr", [D, T], BF16)

    ARENA = 206 * 1024
    arena = nc.alloc_sbuf_tensor("arena", [128, ARENA], U8)
    psum = nc.alloc_psum_tensor("psum", [128, 8, 512], F32)

    def view(off, nbytes, dt, **re):
        v = arena[:, off:off + nbytes].bitcast(dt)
        return v

    def v3(off, a, b, dt):
        sz = 4 if dt == F32 else 2
        v = arena[:, off:off + a * b * sz].bitcast(dt)
        return v.rearrange("p (a b) -> p a b", a=a)

    def v2(off, n, dt):
        sz = 4 if dt == F32 else 2
        return arena[:, off:off + n * sz].bitcast(dt)

    o = 0
    H_OFF = o; o += 4 * D * 4
    XN_OFF = o; o += NCH * TT * 2
    AC_OFF = o; o += FCH * TT * 2
    CHAIN_END = o
    WR_OFF = o; o += NSLOT * 4096
    SIL_OFF = o; o += 8192
    YS_OFF = o; o += 4096
    STG_OFF = o; o += 8192
    CF_OFF = o; o += 3 * 128 * 4
    ONB_OFF = o; o += 128 * 2
    GN_OFF = o; o += 14 * NCH * 4
    GSUB_OFF = o; o += 32
    SS_OFF = o; o += 64 * 4
    LAM_OFF = o; o += 64
    JUNK_OFF = o; o += 2 * 2048
    assert o <= ARENA, o

    h_sb = v3(H_OFF, 4, D, F32)
    xnT = v3(XN_OFF, NCH, TT, BF16)
    actT = v3(AC_OFF, FCH, TT, BF16)
    wring = [v3(WR_OFF + s * 4096, 4, 512, BF16) for s in range(NSLOT)]
    sil = v3(SIL_OFF, 4, 512, F32)
    ys = v3(YS_OFF, 2, 512, F32)
    stg = [v3(STG_OFF + s * 4096, 4, 512, BF16) for s in range(2)]
    cf = v3(CF_OFF, 3, 128, F32)
    ident, Umat, ones32 = cf[:, 0, :], cf[:, 1, :], cf[:, 2, :]
    onesb = v2(ONB_OFF, 128, BF16)
    gn = v2(GN_OFF, 14 * NCH, F32)
    gsub = v2(GSUB_OFF, 8, F32)
    stat = v2(SS_OFF, 64, F32)
    lam_sb = v2(LAM_OFF, 16, F32)
    junks = [v2(JUNK_OFF + i * 2048, 512, F32) for i in range(2)]
    jstate = {"i": 0}

    def square_acc(s, g):
        ji = jstate["i"]; jstate["i"] = 1 - ji
        col = s * 8 + g
        ACT(junks[ji], h_sb[:, s, g * 512:(g + 1) * 512], AF.Square, reads=[("h", g)],
            writes=[("junk", ji), ("stat", col)], accum_out=stat[:, col:col + 1])
    gfin = v2(XN_OFF, D, F32)

    state = {"wslot": 0, "stg": 0, "psset": 0}

    def ACT(out, in_, func, reads, writes, **kw):
        P.add("act", lambda e: e.activation(out=out, in_=in_, func=func, **kw), reads=reads, writes=writes)

    def STT(out, in0, scalar, in1, op0, op1, reads, writes):
        P.add("dve", lambda e: e.scalar_tensor_tensor(out=out, in0=in0, scalar=scalar, in1=in1, op0=op0, op1=op1),
              reads=reads, writes=writes)

    def TTOP(out, in0, in1, op, reads, writes):
        P.add("dve", lambda e: e.tensor_tensor(out=out, in0=in0, in1=in1, op=op), reads=reads, writes=writes)

    def TS(out, in0, s1, s2, op0, op1, reads, writes):
        if s2 is None:
            P.add("dve", lambda e: e.tensor_scalar(out=out, in0=in0, scalar1=s1, scalar2=None, op0=op0),
                  reads=reads, writes=writes)
        else:
            P.add("dve", lambda e: e.tensor_scalar(out=out, in0=in0, scalar1=s1, scalar2=s2, op0=op0, op1=op1),
                  reads=reads, writes=writes)

    def PTT(out, in0, in1, op, reads, writes):
        P.add("pc", lambda e: e.tensor_tensor(out=out, in0=in0, in1=in1, op=op), reads=reads, writes=writes)

    def PCOPY(out, in_, reads, writes):
        P.add("pc", lambda e: e.tensor_copy(out=out, in_=in_), reads=reads, writes=writes)

    def RECIP(out, in_, reads, writes):
        P.add("dve", lambda e: e.reciprocal(out=out, in_=in_), reads=reads, writes=writes)

    def COPY(out, in_, reads, writes):
        P.add("dve", lambda e: e.tensor_copy(out=out, in_=in_), reads=reads, writes=writes)

    def MM(out, lhsT, rhs, start, stop, reads, writes):
        P.add("pe", lambda e: e.matmul(out, lhsT, rhs, start=start, stop=stop), reads=reads, writes=writes)

    def MMS(lst, reads, writes):
        def fn(e):
            ins = None
            for (o_, l_, r_, st_, sp_) in lst:
                ins = e.matmul(o_, l_, r_, start=st_, stop=sp_)
            return ins
        P.add("pe", fn, reads=reads, writes=writes)

    def DMA(q, out, in_, reads=(), writes=()):
        P.add(q, lambda e: e.dma_start(out=out, in_=in_), reads=reads, writes=writes)

    def wload(src_ap):
        s = state["wslot"]; state["wslot"] = (s + 1) % NSLOT
        dst = wring[s]
        src = src_ap.rearrange("(c p) n -> p c n", p=128)
        P.add("pool", lambda e, dst=dst, src=src: e.dma_start(out=dst, in_=src), writes=[("w", s)])
        return s

    def next_banks():
        b = state["psset"]; state["psset"] = 1 - b
        return b * 4

    def gemm_feat(src, srckey, nch, W, col0, ncols, evac):
        for g in range(ncols // 512):
            b0 = next_banks()
            c0 = col0 + g * 512
            for cq in range(nch // 4):
                s = wload(W[cq * 512:(cq + 1) * 512, c0:c0 + 512])

                def fn(e, s=s, cq=cq, b0=b0):
                    ins = None
                    for ci in range(4):
                        c = cq * 4 + ci
                        for jj in range(4):
                            ins = e.matmul(psum[:, b0 + jj, :], wring[s][:, ci, jj * 128:(jj + 1) * 128],
                                           src[:, c, :], start=(c == 0), stop=(c == nch - 1))
                    return ins
                P.add("pe", fn, reads=[("w", s), srckey], writes=[("ps", b0 + j) for j in range(4)])
            evac(g, b0)

    def gemm_tok(src, srckey, nch, W, col0, ncols, evac):
        for g in range(ncols // 512):
            b0 = next_banks()
            c0 = col0 + g * 512
            for cq in range(nch // 4):
                s = wload(W[cq * 512:(cq + 1) * 512, c0:c0 + 512])

                def fn(e, s=s, cq=cq, b0=b0):
                    ins = None
                    for ci in range(4):
                        c = cq * 4 + ci
                        for st in range(4):
                            ins = e.matmul(psum[:, b0 + st, :], src[:, c, st * 128:(st + 1) * 128],
                                           wring[s][:, ci, :], start=(c == 0), stop=(c == nch - 1))
                    return ins
                P.add("pe", fn, reads=[("w", s), srckey], writes=[("ps", b0 + j) for j in range(4)])
            evac(g, b0)

    def pskeys(b0, n=4):
        return [("ps", b0 + j) for j in range(n)]

    def evac_residual(alpha):
        def ev(g, b0):
            hv = h_sb[:, :, g * 512:(g + 1) * 512]
            P.add("dve", lambda e: e.scalar_tensor_tensor(out=hv, in0=psum[:, b0:b0 + 4, :], scalar=float(alpha),
                                                          in1=hv, op0=ALU.mult, op1=ALU.add),
                  reads=pskeys(b0) + [("h", g)], writes=[("h", g)])
            for s in range(4):
                square_acc(s, g)
        return ev

    evac_toggle = {"i": 0}

    def evac_store_feat(dstT, tile):
        def ev(g, b0):
            si = state["stg"]; state["stg"] = 1 - si
            sb = stg[si]
            eng = "act" if (evac_toggle["i"] % 2 == 0) else "dve"
            evac_toggle["i"] += 1
            if eng == "act":
                P.add("act", lambda e: e.activation(out=sb, in_=psum[:, b0:b0 + 4, :], func=AF.Copy),
                      reads=pskeys(b0), writes=[("stg", si)])
            else:
                P.add("dve", lambda e: e.tensor_copy(out=sb, in_=psum[:, b0:b0 + 4, :]),
                      reads=pskeys(b0), writes=[("stg", si)])
            dst = dstT[g * 512:(g + 1) * 512, tile * TT:(tile + 1) * TT].rearrange("(j p) t -> p j t", p=128)
            P.add("sp", lambda e: e.dma_start(out=dst, in_=sb), reads=[("stg", si)], writes=[("dram", dstT.name, g, tile)])
        return ev

    def evac_store_tok(dst, tile):
        def ev(g, b0):
            si = state["stg"]; state["stg"] = 1 - si
            sb = stg[si]
            eng = "act" if (evac_toggle["i"] % 2 == 0) else "dve"
            evac_toggle["i"] += 1
            if eng == "act":
                P.add("act", lambda e: e.activation(out=sb, in_=psum[:, b0:b0 + 4, :], func=AF.Copy),
                      reads=pskeys(b0), writes=[("stg", si)])
            else:
                P.add("dve", lambda e: e.tensor_copy(out=sb, in_=psum[:, b0:b0 + 4, :]),
                      reads=pskeys(b0), writes=[("stg", si)])
            d = dst[tile * TT:(tile + 1) * TT, g * 512:(g + 1) * 512].rearrange("(s p) c -> p s c", p=128)
            P.add("sp", lambda e: e.dma_start(out=d, in_=sb), reads=[("stg", si)], writes=[("dram", dst.name, g, tile)])
        return ev

    HKEYS = [("h", g) for g in range(8)]

    def norm_stage(gidx, final=False, presq=True):
        if not presq:
            for s in range(4):
                for g in range(8):
                    square_acc(s, g)
        P.add("dve", lambda e: e.tensor_reduce(out=stat[:, 32:36], in_=stat[:, 0:32].rearrange("p (s g) -> p s g", s=4),
                                               axis=AX.X, op=ALU.add),
              reads=[("stat", c) for c in range(32)], writes=[("stat", "ss")])
        P.add("dve", lambda e: e.tensor_scalar(out=stat[:, 36:40], in0=stat[:, 32:36], scalar1=1.0 / D, scalar2=EPS,
                                               op0=ALU.mult, op1=ALU.add),
              reads=[("stat", "ss")], writes=[("stat", "ms")])
        P.add("act", lambda e: e.activation(out=stat[:, 40:44], in_=stat[:, 36:40], func=AF.Sqrt),
              reads=[("stat", "ms")], writes=[("stat", "sd")])
        P.add("dve", lambda e: e.reciprocal(out=stat[:, 44:48], in_=stat[:, 40:44]),
              reads=[("stat", "sd")], writes=[("stat", "rstd")])
        if final:
            for s in range(4):
                P.add("dve", lambda e, s=s: e.scalar_tensor_tensor(
                    out=h_sb[:, s, :], in0=h_sb[:, s, :], scalar=stat[:, 44 + s:45 + s], in1=gfin,
                    op0=ALU.mult, op1=ALU.mult),
                    reads=HKEYS + [("stat", "rstd"), ("gfin",)], writes=HKEYS)
            return
        k = 0
        for s in range(4):
            for g in range(8):
                yi = k % 2
                bank = 0 if (k % 2 == 0) else 4
                k += 1
                P.add("act", lambda e, s=s, g=g, yi=yi: e.activation(
                    out=ys[:, yi, :], in_=h_sb[:, s, g * 512:(g + 1) * 512], func=AF.Copy,
                    scale=stat[:, 44 + s:45 + s]),
                    reads=[("h", g), ("stat", "rstd")], writes=[("ys", yi)])

                def tfn(e, yi=yi, bank=bank):
                    ins = None
                    for jj in range(4):
                        ins = e.transpose(out=psum[:, bank, jj * 128:(jj + 1) * 128],
                                          in_=ys[:, yi, jj * 128:(jj + 1) * 128], identity=ident)
                    return ins
                P.add("pe", tfn, reads=[("ys", yi), ("cf",)], writes=[("ps", bank)])
                for jj in range(4):
                    c = g * 4 + jj
                    P.add("dve", lambda e, c=c, s=s, jj=jj, bank=bank: e.tensor_scalar(
                        out=xnT[:, c, s * 128:(s + 1) * 128], in0=psum[:, bank, jj * 128:(jj + 1) * 128],
                        scalar1=gn[:, gidx * NCH + c:gidx * NCH + c + 1], scalar2=None, op0=ALU.mult),
                        reads=[("ps", bank), ("gn",)], writes=[("xnT",)])

    def ffn_half(l, i, presq=True):
        norm_stage(l * 2 + i, presq=presq)
        Win = w_ffn_in[l, i]
        Wout = w_ffn_out[l, i]
        for jg in range(FCH // 4):
            def ev_gate(g, b0):
                P.add("act", lambda e: e.activation(out=sil, in_=psum[:, b0:b0 + 4, :], func=AF.Silu),
                      reads=pskeys(b0), writes=[("sil",)])
            gemm_feat(xnT, ("xnT",), NCH, Win, jg * 512, 512, ev_gate)

            def ev_up(g, b0, jg=jg):
                P.add("dve", lambda e: e.tensor_tensor(out=actT[:, jg * 4:(jg + 1) * 4, :], in0=psum[:, b0:b0 + 4, :],
                                                       in1=sil, op=ALU.mult),
                      reads=pskeys(b0) + [("sil",)], writes=[("actT",)])
            gemm_feat(xnT, ("xnT",), NCH, Win, FF + jg * 512, 512, ev_up)
        gemm_tok(actT, ("actT",), FCH, Wout, 0, D, evac_residual(0.5))

    def load_h(src, tile):
        s_ap = src[tile * TT:(tile + 1) * TT, :].rearrange("(s p) d -> p s d", p=128)
        P.add("sp", lambda e: e.dma_start(out=h_sb, in_=s_ap), reads=[("dram", src.name, "h", tile)], writes=HKEYS)

    def store_h(dst, tile):
        d_ap = dst[tile * TT:(tile + 1) * TT, :].rearrange("(s p) d -> p s d", p=128)
        P.add("sp", lambda e: e.dma_start(out=d_ap, in_=h_sb), reads=HKEYS, writes=[("dram", dst.name, "h", tile)])

    def load_oT(tile):
        s_ap = oT_d[:, tile * TT:(tile + 1) * TT].rearrange("(c p) t -> p c t", p=128)
        P.add("sp", lambda e: e.dma_start(out=xnT, in_=s_ap), reads=[("dram", "oT")], writes=[("xnT",), ("gfin",)])

    def oproj(W, tile):
        if tile == 0:
            load_oT(tile)
        gemm_tok(xnT, ("xnT",), NCH, W, 0, D, evac_residual(1.0))

    def next_oT(tile):
        if tile + 1 < NT:
            load_oT(tile + 1)

    P.add("sp", lambda e: e.dma_start(out=arena[:, CF_OFF:CF_OFF + 3 * 128 * 4].bitcast(F32), in_=cf32_d), writes=[("cf",)])
    P.add("sp", lambda e: e.dma_start(out=gn, in_=gains_d), writes=[("gn",)])
    P.add("sp", lambda e: e.dma_start(out=gsub[:, 0:4], in_=gsub_d), writes=[("gsub",)])
    P.add("dve", lambda e: e.memset(onesb, 1.0), writes=[("onesb",)])

    a = 0
    AQ_OFF = a; a += 2 * 2 * T * 2
    AK_OFF = a; a += 2 * 2 * T * 2
    AV_OFF = a; a += 2 * NB * 256 * 2
    AO_OFF = a; a += 2 * 2 * T * 2
    MSK_OFF = a; a += 8 * 512 * 4
    TMP_OFF = a; a += 4 * 3 * 2048
    AB_OFF = a; a += 4 * 1024
    SSUM_OFF = a; a += 4 * 2048
    R_OFF = a; a += 3 * 2048
    OF_OFF = a; a += 2 * 2048
    SQ_OFF = a; a += 2 * 2048
    assert a <= CHAIN_END, a

    aq = [v3(AQ_OFF + b * 2 * T * 2, 2, T, BF16) for b in range(2)]
    ak = [v3(AK_OFF + b * 2 * T * 2, 2, T, BF16) for b in range(2)]
    av = [v3(AV_OFF + b * NB * 256 * 2, NB, 256, BF16) for b in range(2)]
    ao = [v3(AO_OFF + b * 2 * T * 2, 2, T, BF16) for b in range(2)]
    msk = v3(MSK_OFF, 8, 512, F32)
    tmp = [[v2(TMP_OFF + (p * 3 + i) * 2048, 512, F32) for i in range(3)] for p in range(4)]
    abf = [v2(AB_OFF + p * 1024, 512, BF16) for p in range(4)]
    ssums = [v2(SSUM_OFF + p * 2048, 512, F32) for p in range(4)]
    rbuf = [v2(R_OFF + i * 2048, 512, F32) for i in range(3)]
    ofp = v3(OF_OFF, 2, 512, F32)
    sqf = v3(SQ_OFF, 2, 512, F32)

    def load_masks(which):
        if which == "a":
            DMA("sp", arena[:, MSK_OFF:MSK_OFF + 8 * 512 * 4].bitcast(F32), cmask_d[:, 0:8 * 512], writes=[("msk",)])
        else:
            DMA("sp", arena[:, MSK_OFF:MSK_OFF + 5 * 512 * 4].bitcast(F32), cmask_d[:, 8 * 512:13 * 512], writes=[("msk",)])

    def attn_a(l):
        P.barrier()
        load_masks("a")
        H = 32
        NS = 4

        def load_head(h, b):
            DMA("sp", aq[b][:, 0, :], qT_d[h * 128:(h + 1) * 128, :], writes=[("aq", b)])
            DMA("sp", ak[b][:, 0, :], kT_d[h * 128:(h + 1) * 128, :], writes=[("ak", b)])
            DMA("sp", av[b][:, :, 0:128], v_d[:, h * 128:(h + 1) * 128].rearrange("(b p) d -> p b d", p=128),
                writes=[("av", b)])

        def stream(h, qt, slot):
            b = h % 2
            kbs = list(range(4 * qt + 3, -1, -1))
            zc = 2 * slot
            ob = 2 * slot + 1
            E, SPb, Wb = tmp[slot]
            A = abf[slot]
            ssum = ssums[slot]
            zps = psum[:, zc, :]
            qsl = aq[b][:, 0, qt * TT:(qt + 1) * TT]
            kE, kSP, kW, kA, kS = ("E", slot), ("SP", slot), ("W", slot), ("A", slot), ("ssum", slot)
            for idx, kb in enumerate(kbs):
                diag = kb >= 4 * qt
                oi = kb - 4 * qt
                last = idx == len(kbs) - 1
                MM(zps, ak[b][:, 0, kb * 128:(kb + 1) * 128], qsl, True, True,
                   reads=[("ak", b), ("aq", b)], writes=[("ps", zc)])
                yield
                ACT(E, zps, AF.Exp, reads=[("ps", zc)], writes=[kE], scale=SCALE)
                yield
                ACT(SPb, E, AF.Ln, reads=[kE], writes=[kSP], bias=1.0)
                yield
                STT(Wb, zps, SCALE, SPb, ALU.mult, ALU.subtract, reads=[("ps", zc), kSP], writes=[kW])
                if diag:
                    TTOP(SPb, SPb, msk[:, oi, :], ALU.mult, reads=[kSP, ("msk",)], writes=[kSP])
                    TTOP(Wb, Wb, msk[:, 4 + oi, :], ALU.add, reads=[kW, ("msk",)], writes=[kW])
                yield
                if idx == 0:
                    MM(zps, Umat, SPb, True, True, reads=[kSP, ("cf",)], writes=[("ps", zc)])
                else:
                    MMS([(zps, Umat, SPb, True, False), (zps, ones32, ssum, False, True)],
                        reads=[kSP, kS, ("cf",)], writes=[("ps", zc)])
                yield
                TTOP(Wb, Wb, zps, ALU.subtract, reads=[kW, ("ps", zc)], writes=[kW])
                if not last:
                    if idx == 0:
                        COPY(ssum, SPb, reads=[kSP], writes=[kS])
                    else:
                        TTOP(ssum, ssum, SPb, ALU.add, reads=[kSP, kS], writes=[kS])
                yield
                ACT(A, Wb, AF.Exp, reads=[kW], writes=[kA])
                yield
                MM(psum[:, ob, :], av[b][:, kb, 0:128], A, idx == 0, last,
                   reads=[("av", b), kA], writes=[("ps", ob)])
                yield
            ACT(ao[b][:, 0, qt * TT:(qt + 1) * TT], psum[:, ob, :], AF.Copy, reads=[("ps", ob)], writes=[("ao", b, qt)])
            yield

        queue = [(h, qt) for h in range(H) for qt in range(NT)]
        queue.reverse()
        active = [None] * NS
        meta = [None] * NS
        done_cnt = {}
        load_head(0, 0)
        load_head(1, 1)
        loaded = {0, 1}
        while queue or any(a_ is not None for a_ in active):
            for slot in range(NS):
                if active[slot] is None and queue and queue[-1][0] in loaded:
                    h, qt = queue.pop()
                    active[slot] = stream(h, qt, slot)
                    meta[slot] = h
                if active[slot] is not None:
                    try:
                        next(active[slot])
                    except StopIteration:
                        h = meta[slot]
                        active[slot] = None
                        done_cnt[h] = done_cnt.get(h, 0) + 1
                        if done_cnt[h] == NT:
                            b = h % 2
                            DMA("sp", oT_d[h * 128:(h + 1) * 128, :], ao[b][:, 0, :],
                                reads=[("ao", b, q_) for q_ in range(NT)], writes=[("dram", "oT")])
                            if h + 2 < H:
                                load_head(h + 2, b)
                                loaded.add(h + 2)
        P.barrier()

    def attn_b(l):
        j = l - 2
        lam_init = lam_init_of(l)
        P.barrier()
        load_masks("b")
        lrep = v3(TMP_OFF, 4, 128, F32)
        LK = ("U", 0)
        DMA("sp", lrep, lamrep_d[:, j * 512:(j + 1) * 512].rearrange("p (w i) -> p w i", w=4), writes=[LK])
        TTOP(lrep[:, 0, :], lrep[:, 0, :], lrep[:, 1, :], ALU.mult, reads=[LK], writes=[LK])
        TTOP(lrep[:, 2, :], lrep[:, 2, :], lrep[:, 3, :], ALU.mult, reads=[LK], writes=[LK])
        P.add("dve", lambda e: e.tensor_reduce(out=lam_sb[:, 0:1], in_=lrep[:, 0, :], axis=AX.X, op=ALU.add),
              reads=[LK], writes=[("lam", 0)])
        P.add("dve", lambda e: e.tensor_reduce(out=lam_sb[:, 1:2], in_=lrep[:, 2, :], axis=AX.X, op=ALU.add),
              reads=[LK], writes=[("lam", 1)])
        ACT(lam_sb[:, 2:4], lam_sb[:, 0:2], AF.Exp, reads=[("lam", 0), ("lam", 1)], writes=[("lam", 2)])
        TTOP(lam_sb[:, 4:5], lam_sb[:, 2:3], lam_sb[:, 3:4], ALU.subtract, reads=[("lam", 2)], writes=[("lam", 4)])
        TS(lam_sb[:, 5:6], lam_sb[:, 4:5], float(lam_init), None, ALU.add, None, reads=[("lam", 4)], writes=[("lam", 5)])
        lam_ap = lam_sb[:, 5:6]
        H = 16

        def load_head(h, b):
            DMA("sp", aq[b], qT_d[h * 256:(h + 1) * 256, :].rearrange("(c p) t -> p c t", p=128), writes=[("aq", b)])
            DMA("sp", ak[b], kshT_d[h * 256:(h + 1) * 256, :].rearrange("(c p) t -> p c t", p=128), writes=[("ak", b)])
            DMA("sp", av[b], vsh_d[:, h * 256:(h + 1) * 256].rearrange("(b p) d -> p b d", p=128), writes=[("av", b)])
        load_head(0, 0)
        r1, r2l, rstd = rbuf
        ucnt = 0
        for h in range(H):
            b = h % 2
            slope = 2.0 ** (-8.0 * (h + 1) / 16.0)
            ch = -slope / SCALE
            if h + 1 < H:
                load_head(h + 1, 1 - b)
            for qt in range(NT):
                kbs = list(range(4 * qt + 3, -1, -1))
                units = [(idx, kb, c) for idx, kb in enumerate(kbs) for c in range(2)]
                pend = []

                def front(idx, kb, c, ui):
                    diag = kb >= 4 * qt
                    oi = kb - 4 * qt
                    delta = qt * TT - kb * 128
                    zb = c
                    Ub = tmp[ui % 4][0]
                    Eb = abf[ui % 4]
                    zps = psum[:, zb, :]
                    MM(zps, ak[b][:, c, kb * 128:(kb + 1) * 128], aq[b][:, c, qt * TT:(qt + 1) * TT], True, True,
                       reads=[("ak", b), ("aq", b)], writes=[("ps", zb)])
                    mt = msk[:, 1 + oi, :] if diag else msk[:, 0, :]
                    STT(Ub, mt, float(ch), zps, ALU.mult, ALU.add, reads=[("ps", zb), ("msk",)], writes=[("U", ui % 4)])
                    bconst = 0.0 if diag else float(-slope * delta)
                    ACT(Eb, Ub, AF.Exp, reads=[("U", ui % 4)], writes=[("Eb", ui % 4)], scale=SCALE, bias=bconst)

                def back(idx, kb, c, ui):
                    last = idx == len(kbs) - 1
                    Eb = abf[ui % 4]
                    MMS([(psum[:, 2 + 2 * c, :], av[b][:, kb, 0:128], Eb, idx == 0, last),
                         (psum[:, 3 + 2 * c, :], av[b][:, kb, 128:256], Eb, idx == 0, last),
                         (psum[:, 6 + c, :], onesb, Eb, idx == 0, last)],
                        reads=[("av", b), ("Eb", ui % 4), ("onesb",)],
                        writes=[("ps", 2 + 2 * c), ("ps", 3 + 2 * c), ("ps", 6 + c)])

                LAG = 2
                n = len(units)
                uis = []
                for i in range(n + LAG):
                    if i < n:
                        uis.append(ucnt)
                        front(*units[i], ucnt)
                        ucnt += 1
                    if i >= LAG:
                        back(*units[i - LAG], uis[i - LAG])
                RECIP(r1, psum[:, 6, :], reads=[("ps", 6)], writes=[("r", 0)])
                RECIP(r2l, psum[:, 7, :], reads=[("ps", 7)], writes=[("r", 1)])
                TS(r2l, r2l, lam_ap, None, ALU.mult, None, reads=[("r", 1), ("lam", 5)], writes=[("r", 1)])
                for half in range(2):
                    TTOP(ofp[:, half, :], psum[:, 2 + half, :], r1, ALU.mult, reads=[("ps", 2 + half), ("r", 0)],
                         writes=[("of", half)])
                    TTOP(sqf[:, half, :], psum[:, 4 + half, :], r2l, ALU.mult, reads=[("ps", 4 + half), ("r", 1)],
                         writes=[("sq", half)])
                    TTOP(ofp[:, half, :], ofp[:, half, :], sqf[:, half, :], ALU.subtract,
                         reads=[("of", half), ("sq", half)], writes=[("of", half)])
                ACT(sqf, ofp, AF.Square, reads=[("of", 0), ("of", 1), ("sq", 0), ("sq", 1)], writes=[("sq", 0), ("sq", 1)])
                MMS([(psum[:, 0, :], ones32, sqf[:, 0, :], True, False), (psum[:, 0, :], ones32, sqf[:, 1, :], False, True)],
                    reads=[("sq", 0), ("sq", 1), ("cf",)], writes=[("ps", 0)])
                TS(rstd, psum[:, 0, :], 1.0 / 256.0, EPS, ALU.mult, ALU.add, reads=[("ps", 0)], writes=[("r", 2)])
                ACT(rstd, rstd, AF.Sqrt, reads=[("r", 2)], writes=[("r", 2)])
                RECIP(rstd, rstd, reads=[("r", 2)], writes=[("r", 2)])
                TS(rstd, rstd, float(1.0 - lam_init), None, ALU.mult, None, reads=[("r", 2)], writes=[("r", 2)])
                for half in range(2):
                    STT(ao[b][:, half, qt * TT:(qt + 1) * TT], ofp[:, half, :], gsub[:, j * 2 + half:j * 2 + half + 1], rstd,
                        ALU.mult, ALU.mult, reads=[("of", half), ("r", 2), ("gsub",)], writes=[("ao", b)])
            DMA("sp", oT_d[h * 256:(h + 1) * 256, :].rearrange("(c p) t -> p c t", p=128), ao[b], reads=[("ao", b)],
                writes=[("dram", "oT")])
        P.barrier()

    def stop(name):
        return stop_after == name

    dbg_names = []

    def dump(name, tile):
        if not debug:
            return
        if name not in dbg_names:
            dbg_names.append(name)
        dd = nc.dram_tensor("dbg_" + name, [T, D], F32, kind="ExternalOutput").ap() if tile == 0 else dbg_aps[name]
        dbg_aps[name] = dd
        DMA("sp", dd[tile * TT:(tile + 1) * TT, :].rearrange("(s p) d -> p s d", p=128), h_sb, reads=HKEYS,
            writes=[("dram", "dbg", name, tile)])

    dbg_aps = {}

    def finish(tile):
        P.add("sp", lambda e: e.dma_start(out=out_d[tile * TT:(tile + 1) * TT, :].rearrange("(s p) d -> p s d", p=128),
                                          in_=h_sb),
              reads=HKEYS, writes=[("dram", "out", tile)])

    def qkv_a(l, tile):
        W = w_qkv_a[l]
        gemm_feat(xnT, ("xnT",), NCH, W, 0, D, evac_store_feat(qT_d, tile))
        gemm_feat(xnT, ("xnT",), NCH, W, D, D, evac_store_feat(kT_d, tile))
        gemm_tok(xnT, ("xnT",), NCH, W, 2 * D, D, evac_store_tok(v_d, tile))

    def dram_sync(names):
        pass

    def run():
        for tile in range(NT):
            load_h(x_d, tile)
            ffn_half(0, 0, presq=False)
            dump("ffn00", tile)
            if stop("ffn00"):
                finish(tile); continue
            norm_stage(8 + 0)
            qkv_a(0, tile)
            store_h(h_d, tile)
        if stop("ffn00"):
            return
        attn_a(0)
        for tile in range(NT):
            load_h(h_d, tile)
            oproj(w_o_a[0], tile)
            dump("attn0", tile)
            if stop("attn0"):
                finish(tile); continue
            ffn_half(0, 1)
            dump("ffn01", tile)
            ffn_half(1, 0)
            dump("ffn10", tile)
            norm_stage(8 + 1)
            qkv_a(1, tile)
            next_oT(tile)
            store_h(h_d, tile)
        if stop("attn0"):
            return
        attn_a(1)
        for tile in range(NT):
            load_h(h_d, tile)
            oproj(w_o_a[1], tile)
            dump("attn1", tile)
            ffn_half(1, 1)
            dump("ffn11", tile)
            if stop("ffn11"):
                finish(tile); continue
            norm_stage(12)
            gemm_feat(xnT, ("xnT",), NCH, w_kv_b, 0, D, evac_store_feat(kshT_d, tile))
            gemm_tok(xnT, ("xnT",), NCH, w_kv_b, D, D, evac_store_tok(vsh_d, tile))
            ffn_half(2, 0)
            dump("ffn20", tile)
            norm_stage(8 + 2)
            gemm_feat(xnT, ("xnT",), NCH, w_q_b[0], 0, D, evac_store_feat(qT_d, tile))
            next_oT(tile)
            store_h(h_d, tile)
        if stop("ffn11"):
            return
        attn_b(2)
        for tile in range(NT):
            load_h(h_d, tile)
            oproj(w_o_b[0], tile)
            dump("attn2", tile)
            if stop("attn2"):
                finish(tile); continue
            ffn_half(2, 1)
            dump("ffn21", tile)
            ffn_half(3, 0)
            dump("ffn30", tile)
            norm_stage(8 + 3)
            gemm_feat(xnT, ("xnT",), NCH, w_q_b[1], 0, D, evac_store_feat(qT_d, tile))
            next_oT(tile)
            store_h(h_d, tile)
        if stop("attn2"):
            return
        attn_b(3)
        for tile in range(NT):
            load_h(h_d, tile)
            oproj(w_o_b[1], tile)
            dump("attn3", tile)
            ffn_half(3, 1)
            dump("ffn31", tile)
            P.add("sp", lambda e: e.dma_start(out=gfin, in_=gfin_d), reads=[("xnT",)], writes=[("gfin",), ("xnT",)])
            norm_stage(13, final=True)
            next_oT(tile)
            finish(tile)

    run()
    P.barrier()

    sems = {}
    import contextlib
    with contextlib.ExitStack() as es:
        for e in ("pe", "act", "dve", "pc"):
            sems[e] = es.enter_context(nc.semaphore("s_" + e))
        for q in Prog.DMA:
            for r in range(RING):
                sems[(q, r)] = es.enter_context(nc.semaphore(f"s_{q}{r}"))
        block = es.enter_context(nc.Block())
        P.emit(nc, block, sems)
    return nc, P


def make_consts():
    p = np.arange(128)[:, None]
    f = np.arange(512)[None, :]
    ident = np.eye(128, dtype=np.float32)
    U = (np.arange(128)[:, None] > np.arange(128)[None, :]).astype(np.float32)
    ones = np.ones((128, 128), np.float32)
    cf32 = np.concatenate([ident, U, ones], axis=1)
    tiles = []
    for oi in range(4):
        tiles.append((f > p + oi * 128).astype(np.float32))
    for oi in range(4):
        tiles.append(np.where(f > p + oi * 128, 0.0, -30000.0).astype(np.float32))
    tiles.append((f - p).astype(np.float32) + np.zeros((128, 512), np.float32))
    for oi in range(4):
        s = p + oi * 128
        allowed = (s // 64) <= (f // 64)
        tiles.append(np.where(allowed, np.abs(f - s), 1.0e6).astype(np.float32))
    cmask = np.concatenate(tiles, axis=1)
    return np.ascontiguousarray(cf32), np.ascontiguousarray(cmask)


def layout_small(inputs):
    g_all = np.concatenate([
        np.asarray(inputs["ffn_norm"], np.float32).reshape(8, D),
        np.asarray(inputs["attn_norm"], np.float32).reshape(4, D),
        np.asarray(inputs["kv_norm"], np.float32).reshape(1, D),
        np.asarray(inputs["final_norm"], np.float32).reshape(1, D)], axis=0)
    gains = np.ascontiguousarray(g_all.reshape(14, NCH, 128).transpose(2, 0, 1).reshape(128, 14 * NCH))
    gfin = np.ascontiguousarray(np.broadcast_to(np.asarray(inputs["final_norm"], np.float32)[None, :], (128, D)))
    gsub = np.ascontiguousarray(np.asarray(inputs["subln_norm"], np.float32).reshape(2, 2, 128).transpose(2, 0, 1).reshape(128, 4))
    lam = np.stack([np.asarray(inputs[k], np.float32) for k in ("lambda_q1", "lambda_k1", "lambda_q2", "lambda_k2")], axis=1)
    lamrep = np.ascontiguousarray(np.broadcast_to(lam.reshape(1, 2 * 4 * 128), (128, 2 * 4 * 128)))
    return gains, gfin, gsub, lamrep


_CACHE = {}


def run_cores(inputs, NT, n_cores, stop_after=None, trace=False, debug=False):
    key = (NT, stop_after, debug)
    if key not in _CACHE:
        _CACHE[key] = build_program(NT, stop_after, debug)
    nc, _ = _CACHE[key]
    T = NT * TT
    gains, gfin, gsub, lamrep = layout_small(inputs)
    cf32, cmask = make_consts()
    x = np.asarray(inputs["x"], np.float32)
    shared = {
        "w_ffn_in": np.asarray(inputs["w_ffn_in"], np.float32),
        "w_ffn_out": np.asarray(inputs["w_ffn_out"], np.float32),
        "w_qkv_a": np.asarray(inputs["w_qkv_a"], np.float32),
        "w_o_a": np.asarray(inputs["w_o_a"], np.float32),
        "w_kv_b": np.asarray(inputs["w_kv_b"], np.float32),
        "w_q_b": np.asarray(inputs["w_q_b"], np.float32),
        "w_o_b": np.asarray(inputs["w_o_b"], np.float32),
        "gains": gains, "gfin": gfin, "gsub": gsub, "lamrep": lamrep, "cf32": cf32, "cmask": cmask,
    }
    in_maps = []
    for c in range(n_cores):
        m = dict(shared)
        m["x"] = np.ascontiguousarray(x[c, :T])
        in_maps.append(m)
    res = run_bass_kernel_spmd(nc, in_maps, core_ids=list(range(n_cores)), trace=trace)
    return res


def kernel(**inputs):
    res = run_cores(inputs, NT=4, n_cores=N_CORES)
    out = np.stack([np.asarray(r["out"], np.float32) for r in res.results], axis=0)
    return out
```
